# Optimizing a Trainium2 kernel written in Bass

```python
import numpy as np
import jax
import jax.numpy as jnp
from jax import lax

D_MODEL = 1024
BATCH = 8
SEQ = 2048
DEPTH = 2

CTX_LEN = 256
GRID_W = 64
MIX_DIM = D_MODEL
HEAD_DIM = 64
NA_DIM = MIX_DIM // 2
NA_HEADS = NA_DIM // HEAD_DIM
NA_WIN_ROWS = 8
NA_WIN_COLS = 16
POOL_DIM = MIX_DIM // 4
POOL_WINDOWS = (2, 4, 8, 16)
POOL_GROUPS = len(POOL_WINDOWS)
POOL_GROUP_DIM = POOL_DIM // POOL_GROUPS
ML_DIM = MIX_DIM // 4
ML_HEADS = ML_DIM // HEAD_DIM
ML_CHUNK = 64
ML_CONV_W = 5
ML_N_GATES = 4 * ML_HEADS
IN_COLS = 3 * NA_DIM + POOL_DIM + 4 * ML_DIM + ML_N_GATES
PEER_KEYS = 128
PEER_EXPERTS = PEER_KEYS * PEER_KEYS
PEER_HEADS = 8
PEER_TOPK = 16
PEER_DKEY = 256
PEER_BLOCK = 128
ROPE_BASE = 10000.0
EPS = 1e-6

kernel_name = 'hybrid_na_pool_mlstm_peer_dit'


def _rms(x, g):
    xf = x.astype(jnp.float32)
    y = xf * lax.rsqrt(jnp.mean(xf * xf, axis=-1, keepdims=True) + EPS)
    return (y * g.astype(jnp.float32)).astype(x.dtype)


def _heads(u, n_heads):
    return u.reshape(u.shape[:-1] + (n_heads, HEAD_DIM))


def _split_cols(p):
    sizes = (NA_DIM, NA_DIM, NA_DIM, POOL_DIM, ML_DIM, ML_DIM, ML_DIM, ML_DIM, ML_N_GATES)
    offs = np.cumsum(sizes)[:-1].tolist()
    return jnp.split(p, offs, axis=-1)


def _na_latent(q, k, v, k_ctx, v_ctx, rpb):
    B, S, H, d = q.shape
    R = S // GRID_W
    KR = min(NA_WIN_ROWS, R)
    qg = q.reshape(B, R, GRID_W, H, d)
    kg = k.reshape(B, R, GRID_W, H, d)
    vg = v.reshape(B, R, GRID_W, H, d)
    r = jnp.arange(R)
    r0 = jnp.clip(r - KR // 2, 0, R - KR)
    rows = r0[:, None] + jnp.arange(KR)[None, :]
    k_strip = kg[:, rows]
    v_strip = vg[:, rows]
    cq = jnp.arange(GRID_W)
    c0 = jnp.clip(cq - NA_WIN_COLS // 2, 0, GRID_W - NA_WIN_COLS)
    in_win = (cq[None, :] >= c0[:, None]) & (cq[None, :] < c0[:, None] + NA_WIN_COLS)
    br = rows - r[:, None] + (NA_WIN_ROWS - 1)
    bc = jnp.clip(cq[None, :] - cq[:, None] + (NA_WIN_COLS - 1), 0, 2 * NA_WIN_COLS - 2)
    bias = rpb[:, br[:, None, :, None], bc[None, :, None, :]]
    bias = jnp.transpose(bias, (1, 0, 2, 3, 4)).astype(jnp.float32)
    scale = HEAD_DIM ** -0.5
    s_win = jnp.einsum('brwhd,brkxhd->brhwkx', qg, k_strip).astype(jnp.float32) * scale + bias
    s_win = jnp.where(in_win[:, None, :], s_win, -jnp.inf).reshape(B, R, H, GRID_W, KR * GRID_W)
    s_ctx = jnp.einsum('brwhd,bchd->brhwc', qg, k_ctx).astype(jnp.float32) * scale
    p = jax.nn.softmax(jnp.concatenate([s_win, s_ctx], axis=-1), axis=-1).astype(v.dtype)
    p_win = p[..., :KR * GRID_W].reshape(B, R, H, GRID_W, KR, GRID_W)
    p_ctx = p[..., KR * GRID_W:]
    o = jnp.einsum('brhwkx,brkxhd->brwhd', p_win, v_strip) + jnp.einsum('brhwc,bchd->brwhd', p_ctx, v_ctx)
    return o.reshape(B, S, H * d)


def _na_context(q, k, v):
    s = jnp.einsum('bqhd,bkhd->bhqk', q, k).astype(jnp.float32) * HEAD_DIM ** -0.5
    p = jax.nn.softmax(s, axis=-1).astype(v.dtype)
    o = jnp.einsum('bhqk,bkhd->bqhd', p, v)
    return o.reshape(o.shape[:2] + (NA_DIM,))


def _pool_mix(u, pool_w, pool_scale):
    B, T, _ = u.shape
    ug = u.reshape(B, T, POOL_GROUPS, POOL_GROUP_DIM)
    csum = jnp.concatenate([jnp.zeros((B, 1, POOL_GROUPS, POOL_GROUP_DIM), jnp.float32),
                            jnp.cumsum(ug.astype(jnp.float32), axis=1)], axis=1)
    t = jnp.arange(T)
    outs = []
    for gi, w in enumerate(POOL_WINDOWS):
        lo = jnp.clip(t - w // 2, 0, T - 1)
        hi = jnp.clip(t + (w - w // 2 - 1), 0, T - 1)
        total = csum[:, hi + 1, gi] - csum[:, lo, gi]
        cnt = (hi - lo + 1).astype(jnp.float32)[None, :, None]
        outs.append(total / cnt)
    pooled = jnp.stack(outs, axis=2).astype(u.dtype) - ug
    mixed = jnp.einsum('btgc,gcd->btgd', pooled, pool_w)
    return mixed.reshape(B, T, POOL_DIM) * pool_scale


def _dwconv(u, w):
    C = u.shape[-1]
    return lax.conv_general_dilated(u, w[:, None, :].astype(u.dtype), window_strides=(1,), padding='SAME',
                                    dimension_numbers=('NWC', 'WIO', 'NWC'), feature_group_count=C)


def _axial_rope(T):
    t = jnp.arange(T)
    row = (t // GRID_W).astype(jnp.float32)
    col = (t % GRID_W).astype(jnp.float32)
    nf = HEAD_DIM // 4
    inv = ROPE_BASE ** (-jnp.arange(nf, dtype=jnp.float32) / nf)
    ang = jnp.stack([row[:, None] * inv, col[:, None] * inv], axis=1)
    return jnp.cos(ang), jnp.sin(ang)


def _apply_rope(x, cos, sin):
    xs = x.reshape(x.shape[:-1] + (2, 2, HEAD_DIM // 4))
    a, b = xs[..., 0, :], xs[..., 1, :]
    out = jnp.stack([a * cos - b * sin, a * sin + b * cos], axis=-2)
    return out.reshape(x.shape)


def _mlstm_inputs(uq, uk, uv, ug, conv_w, gate_b, rope):
    qk = jax.nn.silu(_dwconv(jnp.concatenate([uq, uk], axis=-1), conv_w))
    q, k = jnp.split(qk, 2, axis=-1)
    to_h = lambda u: jnp.moveaxis(_heads(u, ML_HEADS).astype(jnp.float32), 2, 1)
    q, k, v = to_h(q), to_h(k), to_h(uv)
    if rope is not None:
        q = _apply_rope(q, *rope)
        k = _apply_rope(k, *rope)
    k = k * HEAD_DIM ** -0.5
    g = ug.astype(jnp.float32) + gate_b.astype(jnp.float32)
    g = jnp.moveaxis(g.reshape(g.shape[:2] + (4, ML_HEADS)), (2, 3), (0, 2))
    gates = (g[0], jax.nn.log_sigmoid(g[1]), g[2], jax.nn.log_sigmoid(g[3]))
    return q, k, v, gates


def _mlstm_chunked(q, k, v, log_i, log_f, state):
    B, H, T, d = q.shape
    L = min(ML_CHUNK, T)
    nc = T // L

    def chunks(a):
        return jnp.moveaxis(a.reshape(a.shape[:2] + (nc, L) + a.shape[3:]), 2, 0)

    lower = jnp.tril(jnp.ones((L, L), dtype=bool))

    def step(carry, inp):
        C, n, m = carry
        qc, kc, vc, ic, fc = inp
        b = jnp.cumsum(fc, axis=-1)
        log_d = jnp.where(lower, b[..., :, None] - b[..., None, :] + ic[..., None, :], -jnp.inf)
        inter = b + m[..., None]
        m_t = jnp.maximum(inter, jnp.max(log_d, axis=-1))
        w_intra = jnp.einsum('bhtd,bhsd->bhts', qc, kc) * jnp.exp(log_d - m_t[..., None])
        w_inter = jnp.exp(inter - m_t)
        num = jnp.einsum('bhts,bhsd->bhtd', w_intra, vc) + w_inter[..., None] * jnp.einsum('bhtd,bhde->bhte', qc, C)
        den = jnp.sum(w_intra, axis=-1) + w_inter * jnp.einsum('bhtd,bhd->bht', qc, n)
        h = num / jnp.maximum(jnp.abs(den), jnp.exp(-m_t))[..., None]
        b_last = b[..., -1]
        log_w = b_last[..., None] - b + ic
        m_new = jnp.maximum(b_last + m, jnp.max(log_w, axis=-1))
        w_s = jnp.exp(log_w - m_new[..., None])
        decay = jnp.exp(b_last + m - m_new)
        C_new = decay[..., None, None] * C + jnp.einsum('bhs,bhsd,bhse->bhde', w_s, kc, vc)
        n_new = decay[..., None] * n + jnp.einsum('bhs,bhsd->bhd', w_s, kc)
        return (C_new, n_new, m_new), h

    state, h = lax.scan(step, state, (chunks(q), chunks(k), chunks(v), chunks(log_i), chunks(log_f)))
    return jnp.moveaxis(h, 0, 2).reshape(B, H, T, d), state


def _mlstm_out(h, uo, norm_g):
    hf = h * lax.rsqrt(jnp.mean(h * h, axis=-1, keepdims=True) + EPS)
    hf = jnp.moveaxis(hf, 1, 2).reshape(h.shape[0], h.shape[2], ML_DIM)
    return (hf * norm_g.astype(jnp.float32)).astype(uo.dtype) * jax.nn.sigmoid(uo)


def _token_mix(xn, hn, w_in, ml_gate_b, na_q_g, na_k_g, na_rpb, pool_w, pool_scale, ml_conv, ml_norm_g,
               need_ctx_out):
    lq, lk, lv, lpool, lmq, lmk, lmv, lmo, lmg = _split_cols(xn @ w_in)
    cq, ck, cv, cpool, cmq, cmk, cmv, cmo, cmg = _split_cols(hn @ w_in)
    qn = lambda u: _rms(_heads(u, NA_HEADS), na_q_g)
    kn = lambda u: _rms(_heads(u, NA_HEADS), na_k_g)
    k_ctx, v_ctx = kn(ck), _heads(cv, NA_HEADS)
    na_lat = _na_latent(qn(lq), kn(lk), _heads(lv, NA_HEADS), k_ctx, v_ctx, na_rpb)
    pool_lat = _pool_mix(lpool, pool_w, pool_scale)
    T = xn.shape[1]
    ql, kl, vl, gl = _mlstm_inputs(lmq, lmk, lmv, lmg, ml_conv, ml_gate_b, _axial_rope(T))
    qc, kc, vc, gc = _mlstm_inputs(cmq, cmk, cmv, cmg, ml_conv, ml_gate_b, None)
    B = xn.shape[0]
    zero = (jnp.zeros((B, ML_HEADS, HEAD_DIM, HEAD_DIM), jnp.float32),
            jnp.zeros((B, ML_HEADS, HEAD_DIM), jnp.float32),
            jnp.zeros((B, ML_HEADS), jnp.float32))
    fl = lambda a: jnp.flip(a, axis=2)
    h_cf, st_f = _mlstm_chunked(qc, kc, vc, gc[0], gc[1], zero)
    h_cb, st_b = _mlstm_chunked(fl(qc), fl(kc), fl(vc), fl(gc[2]), fl(gc[3]), zero)
    h_lf, _ = _mlstm_chunked(ql, kl, vl, gl[0], gl[1], st_f)
    h_lb, _ = _mlstm_chunked(fl(ql), fl(kl), fl(vl), fl(gl[2]), fl(gl[3]), st_b)
    ml_lat = _mlstm_out(h_lf + fl(h_lb), lmo, ml_norm_g)
    y_lat = jnp.concatenate([na_lat, pool_lat, ml_lat], axis=-1)
    if not need_ctx_out:
        return y_lat, None
    na_ctx = _na_context(qn(cq), k_ctx, v_ctx)
    pool_ctx = _pool_mix(cpool, pool_w, pool_scale)
    ml_ctx = _mlstm_out(h_cf + fl(h_cb), cmo, ml_norm_g)
    y_ctx = jnp.concatenate([na_ctx, pool_ctx, ml_ctx], axis=-1)
    return y_lat, y_ctx


def _peer(xn, wq, sub_keys, u_tab, v_tab):
    B, T, D = xn.shape
    n = B * T
    tok = xn.reshape(n, D)
    q = (tok @ wq).reshape(n, PEER_HEADS, 2, PEER_DKEY // 2).astype(jnp.float32)
    s = jnp.einsum('nhpd,hpkd->nhpk', q, sub_keys.astype(jnp.float32))
    s1, i1 = lax.top_k(s[:, :, 0], PEER_TOPK)
    s2, i2 = lax.top_k(s[:, :, 1], PEER_TOPK)
    cand_s = (s1[..., :, None] + s2[..., None, :]).reshape(n, PEER_HEADS, PEER_TOPK * PEER_TOPK)
    cand_i = (i1[..., :, None] * PEER_KEYS + i2[..., None, :]).reshape(n, PEER_HEADS, PEER_TOPK * PEER_TOPK)
    top_s, pos = lax.top_k(cand_s, PEER_TOPK)
    idx = jnp.take_along_axis(cand_i, pos, axis=-1)
    gate = jax.nn.softmax(top_s, axis=-1).astype(xn.dtype)
    nb = n // PEER_BLOCK

    def block(args):
        xb, ib, gb = args
        u_e = u_tab[ib]
        v_e = v_tab[ib]
        a = jax.nn.gelu(jnp.einsum('nd,nhkd->nhk', xb, u_e)) * gb
        return jnp.einsum('nhk,nhkd->nd', a, v_e)

    out = lax.map(block, (tok.reshape(nb, PEER_BLOCK, D),
                          idx.reshape(nb, PEER_BLOCK, PEER_HEADS, PEER_TOPK),
                          gate.reshape(nb, PEER_BLOCK, PEER_HEADS, PEER_TOPK)))
    return out.reshape(B, T, D)


def setup_inputs(seed: int = 0) -> dict:
    key = jax.random.key(seed)
    ks = jax.random.split(key, 24)
    D = D_MODEL

    def nrm(k, shape, s):
        return jax.random.normal(k, shape, jnp.float32) * s

    f_init = jnp.linspace(3.0, 6.0, ML_HEADS, dtype=jnp.float32)
    z = jnp.zeros((ML_HEADS,), jnp.float32)
    gate_base = jnp.concatenate([z, f_init, z, f_init])
    return {
        'x': nrm(ks[0], (BATCH, SEQ, D), 1.0),
        'c': nrm(ks[1], (BATCH, D), 1.0),
        'ctx': nrm(ks[2], (BATCH, CTX_LEN, D), 1.0),
        'c_ctx': nrm(ks[3], (D,), 1.0),
        'w_ada': nrm(ks[4], (DEPTH, D, 6 * D), 0.5 * D ** -0.5),
        'b_ada': nrm(ks[5], (DEPTH, 6 * D), 0.02),
        'norm1_g': 1.0 + nrm(ks[6], (DEPTH, D), 0.02),
        'w_in': nrm(ks[7], (DEPTH, D, IN_COLS), D ** -0.5),
        'ml_gate_b': gate_base[None, :] + nrm(ks[8], (DEPTH, ML_N_GATES), 0.1),
        'na_q_g': 1.0 + nrm(ks[9], (DEPTH, HEAD_DIM), 0.02),
        'na_k_g': 1.0 + nrm(ks[10], (DEPTH, HEAD_DIM), 0.02),
        'na_rpb': nrm(ks[11], (DEPTH, NA_HEADS, 2 * NA_WIN_ROWS - 1, 2 * NA_WIN_COLS - 1), 0.1),
        'pool_w': nrm(ks[12], (DEPTH, POOL_GROUPS, POOL_GROUP_DIM, POOL_GROUP_DIM), POOL_GROUP_DIM ** -0.5),
        'pool_scale': 1.0 + nrm(ks[13], (DEPTH, POOL_DIM), 0.02),
        'ml_conv': nrm(ks[14], (DEPTH, ML_CONV_W, 2 * ML_DIM), ML_CONV_W ** -0.5),
        'ml_norm_g': 1.0 + nrm(ks[15], (DEPTH, ML_DIM), 0.02),
        'w_out': nrm(ks[16], (DEPTH, MIX_DIM, D), MIX_DIM ** -0.5),
        'norm2_g': 1.0 + nrm(ks[17], (DEPTH, D), 0.02),
        'peer_wq': nrm(ks[18], (DEPTH, D, PEER_HEADS * PEER_DKEY), D ** -0.5),
        'peer_keys': nrm(ks[19], (DEPTH, PEER_HEADS, 2, PEER_KEYS, PEER_DKEY // 2), (PEER_DKEY // 2) ** -0.5),
        'peer_u': nrm(ks[20], (DEPTH, PEER_EXPERTS, D), D ** -0.5),
        'peer_v': nrm(ks[21], (DEPTH, PEER_EXPERTS, D), 0.25),
    }


def reference(x, c, ctx, c_ctx, w_ada, b_ada, norm1_g, w_in, ml_gate_b, na_q_g, na_k_g, na_rpb, pool_w,
              pool_scale, ml_conv, ml_norm_g, w_out, norm2_g, peer_wq, peer_keys, peer_u, peer_v):
    hc = ctx
    for l in range(DEPTH):
        need_ctx = l < DEPTH - 1
        mod_lat = jax.nn.silu(c) @ w_ada[l] + b_ada[l]
        mod_ctx = jax.nn.silu(c_ctx) @ w_ada[l] + b_ada[l]
        sh1, sc1, g1, sh2, sc2, g2 = jnp.split(mod_lat[:, None, :], 6, axis=-1)
        csh1, csc1, cg1, csh2, csc2, cg2 = jnp.split(mod_ctx[None, None, :], 6, axis=-1)
        xn = _rms(x, norm1_g[l]) * (1.0 + sc1) + sh1
        hn = _rms(hc, norm1_g[l]) * (1.0 + csc1) + csh1
        y_lat, y_ctx = _token_mix(xn, hn, w_in[l], ml_gate_b[l], na_q_g[l], na_k_g[l], na_rpb[l], pool_w[l],
                                  pool_scale[l], ml_conv[l], ml_norm_g[l], need_ctx)
        x = x + g1 * (y_lat @ w_out[l])
        x = x + g2 * _peer(_rms(x, norm2_g[l]) * (1.0 + sc2) + sh2, peer_wq[l], peer_keys[l], peer_u[l], peer_v[l])
        if need_ctx:
            hc = hc + cg1 * (y_ctx @ w_out[l])
            hc = hc + cg2 * _peer(_rms(hc, norm2_g[l]) * (1.0 + csc2) + csh2, peer_wq[l], peer_keys[l],
                                  peer_u[l], peer_v[l])
    return x
```

```python
import math
import numpy as np
from contextlib import ExitStack
import concourse.bass as bass
import concourse.mybir as mybir
from concourse.bass_utils import run_bass_kernel_spmd

F32 = mybir.dt.float32
BF16 = mybir.dt.bfloat16
ALU = mybir.AluOpType
AF = mybir.ActivationFunctionType
AX = mybir.AxisListType

ENGS = ("pe", "act", "dve", "pool", "sp")
NDMASEM = 24
NHW = 16

NT = 2304
TL = 2048
TC = 256
BLK = [(0, 512), (512, 512), (1024, 512), (1536, 512), (2048, 256)]
EPS = 1e-6
LP = 2336
PL0, PC0 = 8, 2072
LC = 2312
CL0, CC0 = 2, 2054
LN8 = math.log(0.125)
SLACK = 3e-4


class Res:
    def __init__(self, name, parent=None):
        self.name = name
        self.parent = parent
        self.kids = {}
        self.w = None
        self.r = {}

    def sub(self, key):
        k = self.kids.get(key)
        if k is None:
            k = Res(f"{self.name}/{key}", self)
            self.kids[key] = k
        return k

    def _related(self):
        out = [self]
        p = self.parent
        while p is not None:
            out.append(p)
            p = p.parent
        st = list(self.kids.values())
        while st:
            k = st.pop()
            out.append(k)
            st.extend(k.kids.values())
        return out


class T:
    def __init__(self, t, name):
        self.t = t
        self.res = Res(name)

    def __getitem__(self, idx):
        return self.t[idx]

    def sub(self, key):
        return self.res.sub(key)


class TV:
    def __init__(self, base, c0, w, name):
        self.base = base
        self.c0 = c0
        self.w = w
        self.res = Res(name)

    def __getitem__(self, idx):
        if not isinstance(idx, tuple):
            idx = (idx, slice(None))
        r, c = idx
        a = self.c0 + (c.start or 0)
        b = self.c0 + (self.w if c.stop is None else c.stop)
        return self.base.t[r, a:b]


class RV:
    def __init__(self, base, p0, n):
        self.base = base
        self.p0 = p0
        self.n = n
        self.res = base.res

    def __getitem__(self, idx):
        r, c = idx
        return self.base.t[self.p0:self.p0 + self.n, c]


class FW:
    def __init__(self, nc, stack):
        self.nc = nc
        self.stack = stack
        self.rec = {e: [] for e in ENGS}
        self.cnt = {e: 0 for e in ENGS}
        self.esem = {e: stack.enter_context(nc.semaphore(f"s_{e}")) for e in ENGS}
        self.dsem = [stack.enter_context(nc.semaphore(f"d_{i}")) for i in range(NDMASEM)]
        self.duse = [0] * NDMASEM
        self.drr = 0
        self.drr_sw = 0
        self.known = {e: {} for e in ENGS}
        self.uid = 0

    def sb(self, name, shape, dt, stack=None):
        self.uid += 1
        t = (stack or self.stack).enter_context(self.nc.sbuf_tensor(f"{name}_{self.uid}", list(shape), dt))
        return T(t, name)

    def ps(self, name, shape, dt=F32):
        t = self.stack.enter_context(self.nc.psum_tensor(name, list(shape), dt))
        return T(t, name)

    def dram(self, name, shape, dt, kind=None):
        if kind is None:
            t = self.nc.dram_tensor(name, list(shape), dt)
        else:
            t = self.nc.dram_tensor(name, list(shape), dt, kind=kind)
        return T(t, name)

    def _need(self, eng, tok, waits):
        if tok is None:
            return
        sem, val = tok
        if eng == "pe" and sem is self.esem["pe"]:
            return
        kn = self.known[eng]
        if kn.get(sem, 0) >= val:
            return
        kn[sem] = val
        waits[sem] = max(waits.get(sem, 0), val)

    def _deps(self, eng, reads, writes, waits):
        for r in reads:
            for x in r._related():
                self._need(eng, x.w, waits)
        for w in writes:
            for x in w._related():
                self._need(eng, x.w, waits)
                for tk in list(x.r.items()):
                    self._need(eng, tk, waits)

    def _commit(self, tok, reads, writes):
        sem, val = tok
        for r in reads:
            if r.r.get(sem, 0) < val:
                r.r[sem] = val
        for w in writes:
            st = list(w.kids.values())
            while st:
                k = st.pop()
                k.w = None
                k.r = {}
                st.extend(k.kids.values())
            w.w = tok
            w.r = {}

    @staticmethod
    def _res(lst):
        return [getattr(x, "res", x) for x in lst]

    def op(self, eng, fn, reads=(), writes=()):
        reads = self._res(reads)
        writes = self._res(writes)
        waits = {}
        self._deps(eng, reads, writes, waits)
        self.cnt[eng] += 1
        tok = (self.esem[eng], self.cnt[eng])
        self.rec[eng].append((list(waits.items()), fn, (self.esem[eng], 1)))
        self._commit(tok, reads, writes)
        return tok

    def dma(self, eng, out, in_, reads=(), writes=(), **kw):
        reads = self._res(reads)
        writes = self._res(writes)
        waits = {}
        self._deps(eng, reads, writes, waits)
        if eng == "pool":
            k = NHW + self.drr_sw
            self.drr_sw = (self.drr_sw + 1) % (NDMASEM - NHW)
        else:
            k = self.drr
            self.drr = (self.drr + 1) % NHW
        sem = self.dsem[k]
        prev = self.duse[k] * 16
        if prev:
            self._need(eng, (sem, prev), waits)
        self.duse[k] += 1
        tok = (sem, self.duse[k] * 16)

        def fn(e, out=out, in_=in_, kw=kw):
            return e.dma_start(out=out, in_=in_, **kw)

        self.rec[eng].append((list(waits.items()), fn, (sem, 16)))
        self._commit(tok, reads, writes)
        return tok

    def _all_tokens(self):
        toks = [(self.esem[e], self.cnt[e]) for e in ENGS if self.cnt[e]]
        toks += [(self.dsem[k], self.duse[k] * 16) for k in range(NDMASEM) if self.duse[k]]
        return toks

    def barrier(self):
        toks = self._all_tokens()
        for e in ENGS:
            waits = {}
            for tk in toks:
                self._need(e, tk, waits)
            if waits:
                self.rec[e].append((list(waits.items()), None, None))

    def finish_wait(self, eng="sp"):
        waits = {}
        for tk in self._all_tokens():
            self._need(eng, tk, waits)
        if waits:
            self.rec[eng].append((list(waits.items()), None, None))

    def emit(self):
        eng_of = {self.esem[e]: e for e in ENGS}
        waited = {e: set() for e in ENGS}
        for e in ENGS:
            for waits, fn, inc in self.rec[e]:
                for sem, val in waits:
                    if sem in eng_of:
                        waited[eng_of[sem]].add(val)
        rank = {e: {v: i + 1 for i, v in enumerate(sorted(waited[e]))} for e in ENGS}
        with self.nc.Block() as block:
            def run(e, lst, me):
                seq = 0
                for waits, fn, inc in lst:
                    for sem, val in waits:
                        if sem in eng_of:
                            e.wait_ge(sem, rank[eng_of[sem]][val])
                        else:
                            e.wait_ge(sem, val)
                    if fn is not None:
                        if inc[0] is self.esem[me]:
                            seq += 1
                            ins = fn(e)
                            if seq in rank[me]:
                                ins.then_inc(inc[0], 1)
                        else:
                            fn(e).then_inc(inc[0], inc[1])

            @block.tensor
            def _(e):
                run(e, self.rec["pe"], "pe")

            @block.scalar
            def _(e):
                run(e, self.rec["act"], "act")

            @block.vector
            def _(e):
                run(e, self.rec["dve"], "dve")

            @block.gpsimd
            def _(e):
                run(e, self.rec["pool"], "pool")

            @block.sync
            def _(e):
                run(e, self.rec["sp"], "sp")


def MM(o, l, r, st=True, sp=True):
    return lambda e: e.matmul(o, l, r, start=st, stop=sp)


def MMS(o, l, r):
    return lambda e: e.matmul(o, l, r, start=False, stop=True, skip_group_check=True)


def TR(o, i, ident):
    return lambda e: e.transpose(o, i, ident)


def ACT(o, i, f, bias=None, scale=None):
    def fn(e):
        kw = {}
        if bias is not None:
            kw["bias"] = bias
        if scale is not None:
            kw["scale"] = scale
        return e.activation(o, i, f, **kw)
    return fn


def TT(o, a, b, op):
    return lambda e: e.tensor_tensor(o, a, b, op)


def TS(o, a, s1, op0, s2=None, op1=None):
    if op1 is None:
        return lambda e: e.tensor_scalar(o, a, s1, None, op0)
    return lambda e: e.tensor_scalar(o, a, s1, s2, op0, op1)


def STT(o, a, s, b, op0, op1):
    return lambda e: e.scalar_tensor_tensor(o, a, s, b, op0, op1)


def CP(o, i):
    return lambda e: e.tensor_copy(o, i)


def RCP(o, i):
    return lambda e: e.reciprocal(o, i)


def MSET(o, v):
    return lambda e: e.memset(o, v)


def bc(ap, axis, n):
    a = ap.unsqueeze(axis)
    shp = list(a.shape)
    shp[axis] = n
    return a.to_broadcast(shp)


def kview(ap2d):
    return ap2d.rearrange("(k p) n -> p k n", p=128)


class K:
    def __init__(self, nlayers=2, dbg=(), stop=None, with_peer=True):
        self.with_peer = with_peer
        self.nlayers = nlayers
        self.dbg = set(dbg)
        self.stop = stop
        self.nc = bass.Bass("TRN2", target_bir_lowering=False)
        self.outs = {}

    def inp(self, name, shape, dt=F32):
        t = self.fw.dram(name, shape, dt, kind="ExternalInput")
        setattr(self, name, t)
        return t

    def scratch(self, name, shape, dt):
        kind = "ExternalOutput" if name in self.dbg else None
        t = self.fw.dram(name, shape, dt, kind=kind)
        if kind:
            self.outs[name] = t
        return t

    def build(self):
        with ExitStack() as st:
            self.st = st
            fw = self.fw = FW(self.nc, st)
            L = 2
            self.inp("xT", [1024, NT])
            self.inp("cT", [128, 8, 2])
            self.inp("w_ada", [L, 1024, 6144])
            self.inp("bT", [L, 128, 48])
            self.inp("n1g", [L, 128, 8])
            self.inp("n2g", [L, 128, 8])
            self.inp("w_in", [L, 1024, 2832])
            self.inp("w_out", [L, 1024, 1024])
            self.inp("nag", [L, 64, 2])
            self.inp("rpbT", [L, 8, 14, 128, 64])
            self.inp("pool_wbd", [L, 2, 128, 128])
            self.inp("pool_sc", [L, 128, 2])
            self.inp("convw", [L, 64, 8, 5])
            self.inp("ml_gb", [L, 4, 4])
            self.inp("ml_ng", [L, 64, 4])
            self.inp("wq", [L, 1024, 2048])
            self.inp("keysT", [L, 16, 128, 128])
            if self.with_peer:
                self.inp("uT", [L, 1024, 16384])
                self.inp("pv", [L, 16384, 1024])
            self.inp("c_ident", [128, 128])
            self.inp("c_rm", [64, 64])
            self.inp("c_mask", [64, 2, 64])
            self.inp("c_sel", [4, 4, 64])
            self.inp("c_reset", [4, NT])
            self.inp("c_cos", [64, TL])
            self.inp("c_sin", [64, TL])
            self.inp("c_ic", [128, 2, LP])
            self.outT = fw.dram("outT", [1024, TL], F32, kind="ExternalOutput")
            self.outs["outT"] = self.outT

            self.xres = self.scratch("xres", [1024, NT], F32)
            self.yT = self.scratch("yT", [1024, NT], BF16)
            self.Adram = self.scratch("Adram", [32, 128, 18, 512], BF16)
            self.gsc = self.scratch("gsc", [2, 3, 4, NT], F32)
            self.gdec = self.scratch("gdec", [2, 4, 36], F32)

            self.PSD = [fw.ps(f"psd{i}", [128, 1024], F32) for i in range(4)]
            self.PS = [TV(self.PSD[i // 2], (i % 2) * 512, 512, f"ps{i}") for i in range(8)]
            self.ident = fw.sb("ident", [128, 128], F32)
            self.identb = fw.sb("identb", [128, 128], BF16)
            self.ones = fw.sb("ones", [128, 128], F32)
            self.onesb = fw.sb("onesb", [128, 128], BF16)
            self.scT = fw.sb("scT", [128, 8, 2], F32)
            self.modT = fw.sb("modT", [128, 48, 2], F32)
            self.scale1 = fw.sb("scale1", [128, 8, 2], F32)
            self.scale2 = fw.sb("scale2", [128, 8, 2], F32)
            fw.dma("sp", self.ident[:], self.c_ident[:], reads=[self.c_ident], writes=[self.ident])
            fw.op("dve", CP(self.identb[:], self.ident[:]), [self.ident], [self.identb])
            fw.op("pool", MSET(self.ones[:], 1.0), [], [self.ones])
            fw.op("pool", MSET(self.onesb[:], 1.0), [], [self.onesb])
            fw.dma("sp", self.scT[:], self.cT[:], reads=[self.cT], writes=[self.scT])
            fw.op("act", ACT(self.scT[:], self.scT[:], AF.Silu), [self.scT], [self.scT])

            src = self.xT
            for l in range(self.nlayers):
                need_ctx = l < L - 1
                self.layer(l, src, need_ctx)
                src = self.xres
                if self.stop is not None and self.stop[0] == l:
                    break
            fw.finish_wait("sp")
            fw.emit()
        return self.nc

    def stopped(self, l, name):
        return self.stop is not None and self.stop == (l, name)

    def layer(self, l, src, need_ctx):
        fw = self.fw
        last = (l == 1)
        self.phase_mod(l)
        fw.barrier()
        if self.stopped(l, "mod"):
            return
        with ExitStack() as lst:
            xnT = fw.sb("xnT", [128, 8, NT], BF16, lst)
            self.phase_norm(src, self.scale1, self.modT, 0, xnT, None, NT)
            fw.barrier()
            if "xnT_d" in self.dbg:
                d = self.scratch("xnT_d", [1024, NT], BF16)
                fw.dma("sp", kview(d[:, :]), xnT[:], reads=[xnT], writes=[d])
            if self.stopped(l, "norm1"):
                return
            self.phase_na(l, xnT, need_ctx)
            fw.barrier()
            if self.stopped(l, "na"):
                return
            self.phase_pool(l, xnT, need_ctx)
            fw.barrier()
            if self.stopped(l, "pool"):
                return
            self.phase_ml(l, xnT, need_ctx)
            fw.barrier()
            if self.stopped(l, "ml"):
                return
        ntok = NT if need_ctx else TL
        self.phase_out(l, src, ntok)
        fw.barrier()
        if self.stopped(l, "out"):
            return
        self.phase_peer(l, ntok, last)
        fw.barrier()

    def phase_mod(self, l):
        fw = self.fw
        with ExitStack() as ph:
            wa = [fw.sb(f"wa{i}", [128, 8, 512], F32, ph) for i in range(2)]
            bT = fw.sb("bT_s", [128, 48], F32, ph)
            n1 = fw.sb("n1_s", [128, 8], F32, ph)
            n2 = fw.sb("n2_s", [128, 8], F32, ph)
            fw.dma("sp", bT[:], self.bT[l], reads=[self.bT], writes=[bT])
            fw.dma("sp", n1[:], self.n1g[l], reads=[self.n1g], writes=[n1])
            fw.dma("sp", n2[:], self.n2g[l], reads=[self.n2g], writes=[n2])
            psm = self.PS[0]
            for j in range(12):
                w = wa[j % 2]
                fw.dma("sp", w[:], kview(self.w_ada[l, :, j * 512:(j + 1) * 512]), reads=[self.w_ada], writes=[w])
                for fcl in range(4):
                    fc = j * 4 + fcl
                    for k in range(8):
                        fw.op("pe", MM(psm[:, fc * 2:fc * 2 + 2], w[:, k, fcl * 128:(fcl + 1) * 128], self.scT[:, k, :],
                                       k == 0, k == 7), [w, self.scT], [psm])
            fw.op("dve", TT(self.modT[:], psm[:, 0:96].rearrange("p (a b) -> p a b", b=2), bc(bT[:], 2, 2), ALU.add),
                  [psm, bT], [self.modT])
            fw.op("dve", STT(self.scale1[:], self.modT[:, 8:16, :], 1.0, bc(n1[:], 2, 2), ALU.add, ALU.mult),
                  [self.modT, n1], [self.scale1])
            fw.op("dve", STT(self.scale2[:], self.modT[:, 32:40, :], 1.0, bc(n2[:], 2, 2), ALU.add, ALU.mult),
                  [self.modT, n2], [self.scale2])
            if "modT_d" in self.dbg:
                d = self.scratch("modT_d", [128, 96], F32)
                fw.dma("sp", d[:, :], self.modT[:].rearrange("p a b -> p (a b)"), reads=[self.modT], writes=[d])
            fw.barrier()

    def norm_block(self, ph_tiles, src, n0, nb, scale, shoff, out_bf, out_f32, col0):
        for _ in self.norm_block_gen(ph_tiles, src, n0, nb, scale, shoff, out_bf, out_f32, col0):
            pass

    def norm_block_gen(self, ph_tiles, src, n0, nb, scale, shoff, out_bf, out_f32, col0, skip_dma=False, psb=3):
        fw = self.fw
        xb, sq, rs = ph_tiles
        j = 0 if n0 < TL else 1
        ps = self.PS[psb]
        if not skip_dma:
            fw.dma("sp", xb[:, :, :nb], kview(src[:, n0:n0 + nb]), reads=[src], writes=[xb])
        fw.op("act", ACT(sq[:, :, :nb], xb[:, :, :nb], AF.Square), [xb], [sq])
        yield
        for k in range(8):
            onesT = self.onesb if getattr(self, "_norm_bf16", False) else self.ones
            fw.op("pe", MM(ps[:, :nb], onesT[:, :], sq[:, k, :nb], k == 0, k == 7), [onesT, sq], [ps])
        fw.op("act", ACT(rs[:, :nb], ps[:, :nb], AF.Sqrt, bias=EPS, scale=1.0 / 1024.0), [ps], [rs])
        fw.op("dve", RCP(rs[:, :nb], rs[:, :nb]), [rs], [rs])
        yield
        fw.op("dve", TT(xb[:, :, :nb], xb[:, :, :nb], bc(rs[:, :nb], 1, 8), ALU.mult), [xb, rs], [xb])
        yield
        for k in range(8):
            if out_f32 is not None:
                fw.op("act", ACT(out_f32[:, k, :nb], xb[:, k, :nb], AF.Identity,
                                 bias=self.modT[:, shoff + k, j:j + 1], scale=scale[:, k, j:j + 1]),
                      [xb, self.modT, scale], [out_f32])
                fw.op("pool", CP(out_bf[:, k, col0:col0 + nb], out_f32[:, k, :nb]), [out_f32], [out_bf])
                yield
            else:
                fw.op("act", ACT(out_bf[:, k, col0:col0 + nb], xb[:, k, :nb], AF.Identity,
                                 bias=self.modT[:, shoff + k, j:j + 1], scale=scale[:, k, j:j + 1]),
                      [xb, self.modT, scale], [out_bf])

    def phase_norm(self, src, scale, modT, shoff, out_bf, out_f32, ntok):
        fw = self.fw
        with ExitStack() as ph:
            tiles = [(fw.sb("nxb", [128, 8, 512], F32, ph), fw.sb("nsq", [128, 8, 512], BF16, ph),
                      fw.sb("nrs", [128, 512], F32, ph)) for _ in range(2)]
            self._norm_bf16 = True
            for i, (n0, nb) in enumerate(BLK):
                if n0 >= ntok:
                    continue
                self.norm_block(tiles[i % 2], src, n0, nb, scale, shoff, out_bf, out_f32, n0)
            self._norm_bf16 = False
            fw.barrier()

    def proj_fm(self, w, m, xnT, n0, nb, ps):
        fw = self.fw
        for k in range(8):
            fw.op("pe", MM(ps[0:m, :nb], w[:, k, 0:m], xnT[:, k, n0:n0 + nb], k == 0, k == 7), [w, xnT], [ps])

    def rms64(self, src_ps_or_sb, nb, sq, rs, psB, src_res):
        fw = self.fw
        fw.op("act", ACT(sq[0:64, :nb], src_ps_or_sb, AF.Square), [src_res], [sq])
        fw.op("pe", MM(psB[0:64, :nb], self.ones[0:64, 0:64], sq[0:64, :nb]), [self.ones, sq], [psB])
        fw.op("act", ACT(rs[0:64, :nb], psB[0:64, :nb], AF.Sqrt, bias=EPS, scale=1.0 / 64.0), [psB], [rs])
        fw.op("dve", RCP(rs[0:64, :nb], rs[0:64, :nb]), [rs], [rs])

    def phase_na(self, l, xnT, need_ctx):
        fw = self.fw
        PS = self.PS
        with ExitStack() as ph:
            wv = fw.sb("wv", [128, 8, 512], BF16, ph)
            fw.dma("pool", wv[:], kview(self.w_in[l, :, 1024:1536]), reads=[self.w_in], writes=[wv])
            vE = fw.sb("vE", [128, 18, 8, 128], BF16, ph)
            vO = fw.sb("vO", [128, 15, 8, 128], BF16, ph)
            fw.op("pool", MSET(vE[:], 1.0), [], [vE])
            fw.op("pool", MSET(vO[:], 1.0), [], [vO])
            gq = fw.sb("gq", [64, 2], F32, ph)
            fw.dma("sp", gq[:], self.nag[l], reads=[self.nag], writes=[gq])
            fw.op("dve", TS(gq[:, 0:1], gq[:, 0:1], 0.125, ALU.mult), [gq], [gq])
            for ti in range(18 + 15):
                ps = PS[ti % 2]
                t0 = ti * 128 if ti < 18 else 64 + (ti - 18) * 128
                for k in range(8):
                    fw.op("pe", MM(ps[:, :], xnT[:, k, t0:t0 + 128], wv[:, k, :], k == 0, k == 7), [xnT, wv], [ps])
                dst = vE[:, ti, :, 0:64] if ti < 18 else vO[:, ti - 18, :, 0:64]
                fw.op("act", ACT(dst, ps[:, :].rearrange("p (h d) -> p h d", d=64), AF.Copy), [ps], [vE if ti < 18 else vO])
            wqp = [fw.sb(f"wqp{i}", [128, 8, 128], BF16, ph) for i in range(2)]
            wkp = [fw.sb(f"wkp{i}", [128, 8, 128], BF16, ph) for i in range(2)]
            bt = [fw.sb(f"bt{i}", [128, 14, 64], BF16, ph) for i in range(2)]
            qTp = [fw.sb(f"qTp{i}", [128, NT], BF16, ph) for i in range(2)]
            kTp = [fw.sb(f"kTp{i}", [128, NT], BF16, ph) for i in range(2)]
            yh = [fw.sb(f"yh{i}", [64, NT], BF16, ph) for i in range(2)]
            sqs = [fw.sb(f"nsq{i}", [128, 512], BF16, ph) for i in range(2)]
            rss = [fw.sb(f"nrs{i}", [128, 512], F32, ph) for i in range(2)]
            qns = [fw.sb(f"nqn{i}", [128, 512], F32, ph) for i in range(2)]
            nblk = [0]
            bdo = fw.sb("bdones", [128, 128], BF16, ph)
            fw.op("pool", MSET(bdo[:], 0.0), [], [bdo])
            fw.op("pool", MSET(bdo[0:64, 0:64], 1.0), [], [bdo])
            fw.op("pool", MSET(bdo[64:128, 64:128], 1.0), [], [bdo])
            gq2 = fw.sb("gq2", [128, 2], F32, ph)
            fw.dma("sp", gq2[0:64, :], self.nag[l], reads=[self.nag], writes=[gq2])
            fw.dma("sp", gq2[64:128, :], self.nag[l], reads=[self.nag], writes=[gq2])
            fw.op("dve", TS(gq2[:, 0:1], gq2[:, 0:1], 0.125, ALU.mult), [gq2], [gq2])
            ET = [fw.sb(f"ET{i}", [128, 512], BF16, ph) for i in range(4)]
            rd = [fw.sb(f"rd{i}", [128, 256], F32, ph) for i in range(4)]
            for h in range(8):
                b = h % 2
                pair, hh = h // 2, h % 2
                pb = pair % 2
                if hh == 0:
                    fw.dma("pool", wqp[pb][:], kview(self.w_in[l, :, pair * 128:(pair + 1) * 128]), reads=[self.w_in],
                           writes=[wqp[pb]])
                    fw.dma("pool", wkp[pb][:], kview(self.w_in[l, :, 512 + pair * 128:512 + (pair + 1) * 128]),
                           reads=[self.w_in], writes=[wkp[pb]])
                    for (w, dst, gcol, isq) in ((wqp[pb], qTp[pb], 0, True), (wkp[pb], kTp[pb], 1, False)):
                        for (n0, nb) in BLK:
                            if isq and n0 >= TL and not need_ctx:
                                continue
                            i2 = nblk[0] % 2
                            nblk[0] += 1
                            psA, psB = PS[i2], PS[2 + i2]
                            sq, rs, qn = sqs[i2], rss[i2], qns[i2]
                            self.proj_fm(w, 128, xnT, n0, nb, psA)
                            fw.op("act", ACT(sq[:, :nb], psA[:, :nb], AF.Square), [psA], [sq])
                            fw.op("pe", MM(psB[:, :nb], bdo[:, :], sq[:, :nb]), [bdo, sq], [psB])
                            fw.op("act", ACT(rs[:, :nb], psB[:, :nb], AF.Sqrt, bias=EPS, scale=1.0 / 64.0), [psB], [rs])
                            fw.op("dve", RCP(rs[:, :nb], rs[:, :nb]), [rs], [rs])
                            fw.op("dve", TT(qn[:, :nb], psA[:, :nb], rs[:, :nb], ALU.mult), [psA, rs], [qn])
                            fw.op("act", ACT(dst[:, n0:n0 + nb], qn[:, :nb], AF.Identity, scale=gq2[:, gcol:gcol + 1]),
                                  [qn, gq2], [dst])
                fw.dma("pool", bt[b][:], self.rpbT[l, h].rearrange("s p w -> p s w"), reads=[self.rpbT], writes=[bt[b]])
                q, kk, y = RV(qTp[pb], hh * 64, 64), RV(kTp[pb], hh * 64, 64), yh[b]

                def na_qk(r):
                    r0 = min(max(r - 4, 0), 24)
                    delta = r0 - r
                    psS = PS[r % 4]
                    qs = q[:, r * 64:(r + 1) * 64]
                    s0 = delta + 7
                    btv = bt[b][:].rearrange("p (s t) w -> p s t w", t=2)
                    fw.op("pe", MM(psS[:, 0:256].rearrange("p (c w) -> p c w", w=64), self.identb[:, :],
                                   btv[:, s0 // 2:s0 // 2 + 4, s0 % 2, :], True, True), [self.identb, bt[b]], [psS])
                    for c in range(4):
                        k0 = r0 * 64 + c * 128
                        fw.op("pe", MMS(psS[:, c * 64:(c + 1) * 64], kk[:, k0:k0 + 128], qs), [kk, q], [psS])
                    for c in range(2):
                        fw.op("pe", MMS(psS[:, (4 + c) * 64:(5 + c) * 64], kk[:, TL + c * 128:TL + (c + 1) * 128], qs),
                              [kk, q], [psS])

                def na_pv(r):
                    r0 = min(max(r - 4, 0), 24)
                    psS, psO = PS[r % 4], PS[4 + (r % 4)]
                    et, rdd = ET[r % 4], rd[r % 4]
                    fw.op("act", ACT(et[:, 0:384], psS[:, 0:384], AF.Exp), [psS], [et])
                    for c in range(6):
                        if c < 4:
                            k0 = r0 * 64 + c * 128
                            if r0 % 2 == 0:
                                vt, vres = vE[:, k0 // 128, h, :], vE
                            else:
                                vt, vres = vO[:, (k0 - 64) // 128, h, :], vO
                        else:
                            vt, vres = vE[:, 16 + (c - 4), h, :], vE
                        fw.op("pe", MM(psO[:, 0:64], vt, et[:, c * 64:(c + 1) * 64], c == 0, c == 5), [vres, et], [psO])
                    fw.op("dve", RCP(rdd[64:128, 0:64], psO[64:128, 0:64]), [psO], [rdd])
                    fw.op("dve", TT(y[:, r * 64:(r + 1) * 64], psO[0:64, 0:64], rdd[64:128, 0:64], ALU.mult), [psO, rdd], [y])

                na_qk(0)
                for r in range(32):
                    if r + 1 < 32:
                        na_qk(r + 1)
                    na_pv(r)
                if need_ctx:
                    psS, psO = PS[4], PS[6]
                    et, rdd = ET[0], rd[0]
                    for c in range(2):
                        fw.op("pe", MM(psS[:, c * 256:(c + 1) * 256], kk[:, TL + c * 128:TL + (c + 1) * 128], q[:, TL:NT]),
                              [kk, q], [psS])
                    fw.op("act", ACT(et[:, :], psS[:, :], AF.Exp), [psS], [et])
                    for c in range(2):
                        fw.op("pe", MM(psO[0:64, 0:256], vE[:, 16 + c, h, 0:64], et[:, c * 256:(c + 1) * 256],
                                       c == 0, c == 1), [vE, et], [psO])
                    for c in range(2):
                        fw.op("pe", MM(psO[0:64, 256:512], self.onesb[:, 0:64], et[:, c * 256:(c + 1) * 256],
                                       c == 0, c == 1), [self.onesb, et], [psO])
                    fw.op("dve", RCP(rdd[0:64, 0:256], psO[0:64, 256:512]), [psO], [rdd])
                    fw.op("dve", TT(y[:, TL:NT], psO[0:64, 0:256], rdd[0:64, 0:256], ALU.mult), [psO, rdd], [y])
                ncol = NT if need_ctx else TL
                fw.dma("sp", self.yT[h * 64:(h + 1) * 64, 0:ncol], y[:, 0:ncol], reads=[y], writes=[self.yT.sub(("na", h))])
            fw.barrier()

    def phase_pool(self, l, xnT, need_ctx):
        fw = self.fw
        PS = self.PS
        with ExitStack() as ph:
            wp = fw.sb("wp", [128, 8, 256], BF16, ph)
            fw.dma("pool", wp[:], kview(self.w_in[l, :, 1536:1792]), reads=[self.w_in], writes=[wp])
            bd = fw.sb("bd", [128, 2, 128], F32, ph)
            bdb = fw.sb("bdb", [128, 2, 128], BF16, ph)
            fw.dma("sp", bd[:], self.pool_wbd[l].rearrange("c p f -> p c f"), reads=[self.pool_wbd], writes=[bd])
            fw.op("dve", CP(bdb[:], bd[:]), [bd], [bdb])
            psc = fw.sb("psc", [128, 2], F32, ph)
            fw.dma("sp", psc[:], self.pool_sc[l], reads=[self.pool_sc], writes=[psc])
            ic = fw.sb("ic", [128, 2, LP], F32, ph)
            fw.dma("sp", ic[:], self.c_ic[:], reads=[self.c_ic], writes=[ic])
            U = fw.sb("U", [128, 2, LP], F32, ph)
            A2 = fw.sb("A2", [128, 2, LP], F32, ph)
            A4 = fw.sb("A4", [128, 2, LP], F32, ph)
            P = fw.sb("P", [128, 2, LP], F32, ph)
            Pb = fw.sb("Pb", [128, 2, LP], BF16, ph)
            yo = fw.sb("yo", [128, NT], BF16, ph)
            for t in (U, A2, A4, P):
                fw.op("pool", MSET(t[:], 0.0), [], [t])
            for ch in range(2):
                for i, (n0, nb) in enumerate(BLK):
                    ps = PS[i % 2]
                    for k in range(8):
                        fw.op("pe", MM(ps[:, :nb], wp[:, k, ch * 128:(ch + 1) * 128], xnT[:, k, n0:n0 + nb], k == 0, k == 7),
                              [wp, xnT], [ps])
                    off = PL0 + n0 if n0 < TL else PC0
                    fw.op("act", ACT(U[:, ch, off:off + nb], ps[:, :nb], AF.Copy), [ps], [U])
            fw.op("dve", TT(A2[:, :, 1:LP], U[:, :, 0:LP - 1], U[:, :, 1:LP], ALU.add), [U], [A2])
            fw.op("dve", TT(A4[:, :, 1:LP - 1], A2[:, :, 0:LP - 2], A2[:, :, 2:LP], ALU.add), [A2], [A4])
            fw.op("dve", TT(P[0:64, 0, :], A2[0:64, 0, :], ic[0:64, 0, :], ALU.mult), [A2, ic], [P])
            fw.op("dve", TT(P[64:128, 0, :], A4[64:128, 0, :], ic[64:128, 0, :], ALU.mult), [A4, ic], [P])
            A8 = fw.sb("A8", [128, 2, LP], F32, ph)
            A16 = fw.sb("A16", [128, 2, LP], F32, ph)
            fw.op("pool", MSET(A8[:], 0.0), [], [A8])
            fw.op("pool", MSET(A16[:], 0.0), [], [A16])
            fw.op("dve", TT(A8[:, :, 2:LP - 2], A4[:, :, 0:LP - 4], A4[:, :, 4:LP], ALU.add), [A4], [A8])
            fw.op("dve", TT(A16[:, :, 4:LP - 4], A8[:, :, 0:LP - 8], A8[:, :, 8:LP], ALU.add), [A8], [A16])
            fw.op("dve", TT(P[0:64, 1, :], A8[0:64, 1, :], ic[0:64, 1, :], ALU.mult), [A8, ic], [P])
            fw.op("dve", TT(P[64:128, 1, :], A16[64:128, 1, :], ic[64:128, 1, :], ALU.mult), [A16, ic], [P])
            fw.op("dve", TT(Pb[:], P[:], U[:], ALU.subtract), [P, U], [Pb])
            for ch in range(2):
                for i, (n0, nb) in enumerate(BLK):
                    if n0 >= TL and not need_ctx:
                        continue
                    ps = PS[2 + i % 2]
                    off = PL0 + n0 if n0 < TL else PC0
                    fw.op("pe", MM(ps[:, :nb], bdb[:, ch, :], Pb[:, ch, off:off + nb]), [bdb, Pb], [ps])
                    fw.op("act", ACT(yo[:, n0:n0 + nb], ps[:, :nb], AF.Identity, scale=psc[:, ch:ch + 1]), [ps, psc], [yo])
                ncol = NT if need_ctx else TL
                fw.dma("sp", self.yT[512 + ch * 128:512 + (ch + 1) * 128, 0:ncol], yo[:, 0:ncol], reads=[yo],
                       writes=[self.yT.sub(("pool", ch))])
            fw.barrier()

    def phase_ml(self, l, xnT, need_ctx):
        fw = self.fw
        PS = self.PS
        gsc = self.gsc
        with ExitStack() as ph:
            wg = fw.sb("wg", [128, 8, 16], BF16, ph)
            fw.dma("pool", wg[:], kview(self.w_in[l, :, 2816:2832]), reads=[self.w_in], writes=[wg])
            gb = fw.sb("gb", [4, 4], F32, ph)
            fw.dma("sp", gb[:], self.ml_gb[l], reads=[self.ml_gb], writes=[gb])
            rst = fw.sb("rst", [4, NT], F32, ph)
            fw.dma("sp", rst[:], self.c_reset[:], reads=[self.c_reset], writes=[rst])
            G = [fw.sb(f"G{ty}", [4, NT], F32, ph) for ty in range(4)]
            tmp4 = fw.sb("tmp4", [4, NT], F32, ph)
            for ty in range(4):
                for i, (n0, nb) in enumerate(BLK):
                    ps = PS[2 + i % 2]
                    for k in range(8):
                        fw.op("pe", MM(ps[0:4, :nb], wg[:, k, ty * 4:(ty + 1) * 4], xnT[:, k, n0:n0 + nb], k == 0, k == 7),
                              [wg, xnT], [ps])
                    fw.op("act", ACT(G[ty][:, n0:n0 + nb], ps[0:4, :nb], AF.Identity, bias=gb[:, ty:ty + 1]), [ps, gb],
                          [G[ty]])
            for ty in (1, 3):
                fw.op("act", ACT(tmp4[:], G[ty][:], AF.Exp, scale=-1.0), [G[ty]], [tmp4])
                fw.op("act", ACT(tmp4[:], tmp4[:], AF.Ln, bias=1.0), [tmp4], [tmp4])
                fw.op("dve", TS(G[ty][:], tmp4[:], -1.0, ALU.mult), [tmp4], [G[ty]])
            Bc = [fw.sb(f"Bc{d}", [4, NT], F32, ph) for d in range(2)]
            fw.op("dve", lambda e: e.tensor_tensor_scan(Bc[0][:], rst[:], G[1][:], 0.0, ALU.mult, ALU.add), [rst, G[1]],
                  [Bc[0]])
            fw.op("dve", lambda e: e.tensor_tensor_scan(tmp4[:], rst[:], G[3][:], 0.0, ALU.mult, ALU.add), [rst, G[3]],
                  [tmp4])
            v3 = lambda t: t[:].rearrange("p (c s) -> p c s", s=64)
            fw.op("dve", TT(v3(Bc[1]), v3(tmp4)[:, :, 63:64].to_broadcast([4, 36, 64]), v3(tmp4), ALU.subtract), [tmp4],
                  [Bc[1]])
            fw.op("dve", TT(Bc[1][:], Bc[1][:], G[3][:], ALU.add), [Bc[1], G[3]], [Bc[1]])
            eb = fw.sb("EB", [4, NT], F32, ph)
            ak = fw.sb("AK", [4, NT], F32, ph)
            ks = fw.sb("KS", [4, NT], F32, ph)
            dec = fw.sb("DEC", [4, 36], F32, ph)
            for d in range(2):
                gi = G[0] if d == 0 else G[2]
                bl = v3(Bc[d])[:, :, 63:64] if d == 0 else v3(Bc[d])[:, :, 0:1]
                fw.op("act", ACT(eb[:], Bc[d][:], AF.Exp), [Bc[d]], [eb])
                fw.op("dve", TT(ak[:], gi[:], Bc[d][:], ALU.subtract), [gi, Bc[d]], [ak])
                fw.op("dve", TT(v3(ks), v3(ak), bl.to_broadcast([4, 36, 64]), ALU.add), [ak, Bc[d]], [ks])
                fw.op("act", ACT(ak[:], ak[:], AF.Exp, bias=LN8), [ak], [ak])
                fw.op("act", ACT(ks[:], ks[:], AF.Exp, bias=LN8), [ks], [ks])
                fw.op("act", ACT(dec[:].unsqueeze(2), bl, AF.Exp), [Bc[d]], [dec])
                for kind, t in enumerate((eb, ak, ks)):
                    fw.dma("sp", gsc[d, kind], t[:], reads=[t], writes=[gsc.sub((d, kind))])
                fw.dma("sp", self.gdec[d], dec[:], reads=[dec], writes=[self.gdec.sub(d)])
            fw.barrier()
        with ExitStack() as ph:
            wmv = fw.sb("wmv", [128, 8, 256], BF16, ph)
            fw.dma("pool", wmv[:], kview(self.w_in[l, :, 2304:2560]), reads=[self.w_in], writes=[wmv])
            ng = fw.sb("ng", [64, 4], F32, ph)
            fw.dma("sp", ng[:], self.ml_ng[l], reads=[self.ml_ng], writes=[ng])
            cw = fw.sb("cw", [64, 8, 5], F32, ph)
            fw.dma("sp", cw[:], self.convw[l], reads=[self.convw], writes=[cw])
            rm = fw.sb("rm", [64, 64], F32, ph)
            fw.dma("sp", rm[:], self.c_rm[:], reads=[self.c_rm], writes=[rm])
            mask = fw.sb("mask", [64, 2, 64], F32, ph)
            fw.dma("sp", mask[:], self.c_mask[:], reads=[self.c_mask], writes=[mask])
            cos = fw.sb("cos", [64, TL], F32, ph)
            sin = fw.sb("sin", [64, TL], F32, ph)
            fw.dma("sp", cos[:], self.c_cos[:], reads=[self.c_cos], writes=[cos])
            fw.dma("sp", sin[:], self.c_sin[:], reads=[self.c_sin], writes=[sin])
            vt = fw.sb("vth", [64, 36, 64], F32, ph)
            U_ = fw.sb("Ucv", [64, LC], F32, ph)
            fw.op("pool", MSET(U_[:], 0.0), [], [U_])
            acc = fw.sb("acc", [64, LC], F32, ph)
            qc = fw.sb("qc", [64, NT], F32, ph)
            kc = fw.sb("kc", [64, NT], F32, ph)
            t1 = fw.sb("t1", [64, 512], F32, ph)
            t2 = fw.sb("t2", [64, 512], F32, ph)
            qp = fw.sb("qp", [64, NT], F32, ph)
            kp = fw.sb("kp", [64, NT], F32, ph)
            kq = fw.sb("kq", [64, NT], F32, ph)
            ktok = fw.sb("ktok", [64, 36, 64], F32, ph)
            decb = fw.sb("decb", [64, 36], F32, ph)
            HT = fw.sb("HT", [64, NT], F32, ph)
            Ca = [fw.sb(f"Ca{i}", [64, 128], F32, ph) for i in range(2)]
            PT = [fw.sb(f"PT{i}", [64, 64], F32, ph) for i in range(6)]
            dn = [fw.sb(f"dn{i}", [64, 64], F32, ph) for i in range(6)]
            hb = [fw.sb(f"hb{i}", [64, 64], F32, ph) for i in range(6)]
            wmh = [fw.sb(f"wmh{i}", [128, 8, 64], BF16, ph) for i in range(3)]
            yo = fw.sb("yo", [64, NT], BF16, ph)
            sqh = fw.sb("sqh", [64, 512], F32, ph)
            rsh = fw.sb("rsh", [64, 512], F32, ph)
            sg = fw.sb("sgb", [64, 512], F32, ph)
            ncol = NT if need_ctx else TL
            for h in range(4):
                for i, c0 in enumerate((1792, 2048, 2560)):
                    fw.dma("pool", wmh[i][:], kview(self.w_in[l, :, c0 + h * 64:c0 + (h + 1) * 64]), reads=[self.w_in],
                           writes=[wmh[i]])
                for c in range(36):
                    ps = PS[c % 2]
                    for k in range(8):
                        fw.op("pe", MM(ps[0:64, 0:64], xnT[:, k, c * 64:(c + 1) * 64], wmv[:, k, h * 64:(h + 1) * 64],
                                       k == 0, k == 7), [xnT, wmv], [ps])
                    fw.op("act", ACT(vt[:, c, :], ps[0:64, 0:64], AF.Copy), [ps], [vt])
                for (w, dst, ci) in ((wmh[0], qc, h), (wmh[1], kc, 4 + h)):
                    for i, (n0, nb) in enumerate(BLK):
                        off = CL0 + n0 if n0 < TL else CC0
                        ps = PS[2 + i % 2]
                        self.proj_fm(w, 64, xnT, n0, nb, ps)
                        fw.op("act", ACT(U_[:, off:off + nb], ps[0:64, :nb], AF.Copy), [ps], [U_])
                    fw.op("dve", TS(acc[:, 0:LC - 4], U_[:, 0:LC - 4], cw[:, ci, 0:1], ALU.mult), [U_, cw], [acc])
                    for j in range(1, 5):
                        fw.op("dve", STT(acc[:, 0:LC - 4], U_[:, j:LC - 4 + j], cw[:, ci, j:j + 1], acc[:, 0:LC - 4],
                                         ALU.mult, ALU.add), [U_, cw, acc], [acc])
                    fw.op("act", ACT(dst[:, 0:TL], acc[:, 0:TL], AF.Silu), [acc], [dst])
                    fw.op("act", ACT(dst[:, TL:NT], acc[:, CC0 - 2:CC0 - 2 + TC], AF.Silu), [acc], [dst])
                    for i in range(4):
                        n0 = i * 512
                        ps = PS[4 + i % 2]
                        fw.op("pe", MM(ps[0:64, :], rm[:, :], dst[:, n0:n0 + 512]), [rm, dst], [ps])
                        fw.op("dve", TT(t1[:], dst[:, n0:n0 + 512], cos[:, n0:n0 + 512], ALU.mult), [dst, cos], [t1])
                        fw.op("dve", TT(t2[:], ps[0:64, :], sin[:, n0:n0 + 512], ALU.mult), [ps, sin], [t2])
                        fw.op("pool", TT(dst[:, n0:n0 + 512], t1[:], t2[:], ALU.add), [t1, t2], [dst])
                if "mlq_d" in self.dbg and h == 0:
                    d_ = self.scratch("mlq_d", [64, 2, NT], F32)
                    for i, t in enumerate([qc, kc]):
                        fw.dma("sp", d_[:, i, :], t[:], reads=[t], writes=[d_.sub(i)])
                order = [list(range(32, 36)) + list(range(0, 32)), [35, 34, 33, 32] + list(range(31, -1, -1))]
                for d in range(2):
                    for kind, dstT, base in ((0, qp, qc), (1, kp, kc), (2, kq, kc)):
                        fw.dma("sp", dstT[:], gsc[d, kind, h:h + 1, :].to_broadcast([64, NT]), reads=[gsc], writes=[dstT])
                        fw.op("dve" if kind < 2 else "pool", TT(dstT[:], dstT[:], base[:], ALU.mult), [dstT, base], [dstT])
                    fw.dma("sp", decb[:], self.gdec[d, h:h + 1, :].to_broadcast([64, 36]), reads=[self.gdec], writes=[decb])
                    for g8 in range(5):
                        pst = PS[6 + g8 % 2]
                        ncs = 8 if g8 < 4 else 4
                        for cc in range(ncs):
                            c = g8 * 8 + cc
                            fw.op("pe", TR(pst[0:64, cc * 64:(cc + 1) * 64], kq[:, c * 64:(c + 1) * 64], self.ident[0:64, 0:64]),
                                  [kq, self.ident], [pst])
                        fw.op("act", ACT(ktok[:, g8 * 8:g8 * 8 + ncs, :].rearrange("p c s -> p (c s)"), pst[0:64, 0:ncs * 64],
                                         AF.Copy), [pst], [ktok])
                    fw.op("pool", MSET(Ca[0][:], 0.0), [], [Ca[0]])
                    def ml_bank(step):
                        bank = PS[step % 6]
                        psS = TV(bank.base, bank.c0, 64, "x")
                        psN = TV(bank.base, bank.c0 + 64, 128, "x")
                        psC = TV(bank.base, bank.c0 + 192, 128, "x")
                        psS.res = psN.res = psC.res = bank.res
                        return psS, psN, psC

                    def ml_a(step):
                        c = order[d][step]
                        t0 = c * 64
                        psS, psN, psC = ml_bank(step)
                        if need_ctx or c < 32:
                            fw.op("pe", MM(psS[0:64, 0:64], kp[:, t0:t0 + 64], qp[:, t0:t0 + 64]), [kp, qp], [psS])
                            fw.op("dve", TT(PT[step % 6][:], psS[0:64, 0:64], mask[:, d, :], ALU.mult), [psS, mask], [PT[step % 6]])

                    def ml_b(step):
                        c = order[d][step]
                        t0 = c * 64
                        psS, psN, psC = ml_bank(step)
                        cur, nxt = Ca[step % 2], Ca[(step + 1) % 2]
                        pt, dnn, hbb = PT[step % 6], dn[step % 6], hb[step % 6]
                        vv = vt[:, c, :]
                        qch = qp[:, t0:t0 + 64]
                        if need_ctx or c < 32:
                            fw.op("pe", MM(psN[0:64, 0:64], vv, pt[:], True, False), [vt, pt], [psN])
                            fw.op("pe", MM(psN[0:64, 0:64], cur[:, 0:64], qch, False, True), [cur, qp], [psN])
                            fw.op("pe", MM(psN[0:64, 64:128], self.ones[0:64, 0:64], pt[:], True, False), [self.ones, pt],
                                  [psN])
                            fw.op("pe", MM(psN[0:64, 64:128], cur[:, 64:128], qch, False, True), [cur, qp], [psN])
                        if step < 35:
                            fw.op("pe", MM(psC[0:64, 0:64], ktok[:, c, :], vv), [ktok, vt], [psC])
                            fw.op("pe", MM(psC[0:64, 64:128], ktok[:, c, :], self.ones[0:64, 0:64]), [ktok, self.ones], [psC])
                            fw.op("dve", STT(nxt[:], cur[:], decb[:, c:c + 1], psC[0:64, 0:128], ALU.mult, ALU.add),
                                  [cur, decb, psC], [nxt])
                        if need_ctx or c < 32:
                            fw.op("dve", TS(dnn[:], psN[0:64, 64:128], 1.0, ALU.max), [psN], [dnn])
                            fw.op("dve", STT(dnn[:], psN[0:64, 64:128], -1.0, dnn[:], ALU.mult, ALU.max), [psN, dnn], [dnn])
                            fw.op("dve", RCP(dnn[:], dnn[:]), [dnn], [dnn])
                            if d == 0:
                                fw.op("dve", TT(HT[:, t0:t0 + 64], psN[0:64, 0:64], dnn[:], ALU.mult), [psN, dnn], [HT])
                            else:
                                fw.op("dve", TT(hbb[:], psN[0:64, 0:64], dnn[:], ALU.mult), [psN, dnn], [hbb])
                                fw.op("pool", TT(HT[:, t0:t0 + 64], HT[:, t0:t0 + 64], hbb[:], ALU.add), [HT, hbb], [HT])

                    ml_a(0)
                    for step in range(36):
                        if step + 1 < 36:
                            ml_a(step + 1)
                        ml_b(step)
                if "mlh_d" in self.dbg and h == 0:
                    d_ = self.scratch("mlh_d", [64, NT], F32)
                    fw.dma("sp", d_[:, :], HT[:], reads=[HT], writes=[d_])
                for i, (n0, nb) in enumerate(BLK):
                    if n0 >= ncol:
                        continue
                    ps = PS[6]
                    self.proj_fm(wmh[2], 64, xnT, n0, nb, ps)
                    fw.op("act", ACT(sg[:, :nb], ps[0:64, :nb], AF.Sigmoid), [ps], [sg])
                    psB = PS[7]
                    self.rms64(HT[:, n0:n0 + nb], nb, sqh, rsh, psB, HT)
                    fw.op("dve", TT(t1[:, :nb], HT[:, n0:n0 + nb], rsh[0:64, :nb], ALU.mult), [HT, rsh], [t1])
                    fw.op("dve", STT(yo[:, n0:n0 + nb], t1[:, :nb], ng[:, h:h + 1], sg[:, :nb], ALU.mult, ALU.mult),
                          [t1, ng, sg], [yo])
                fw.dma("sp", self.yT[768 + h * 64:768 + (h + 1) * 64, 0:ncol], yo[:, 0:ncol], reads=[yo],
                       writes=[self.yT.sub(("ml", h))])
            fw.barrier()

    def phase_out(self, l, src, ntok):
        fw = self.fw
        PS = self.PS
        with ExitStack() as ph:
            wo = fw.sb("wo", [128, 8, 1024], BF16, ph)
            fw.dma("pool", wo[:], kview(self.w_out[l]), reads=[self.w_out], writes=[wo])
            yb = [fw.sb(f"yb{i}", [128, 8, 512], BF16, ph) for i in range(2)]
            xb = [fw.sb(f"xb{i}", [128, 8, 512], F32, ph) for i in range(2)]
            for i, (n0, nb) in enumerate(BLK):
                if n0 >= ntok:
                    continue
                j = 0 if n0 < TL else 1
                y, x = yb[i % 2], xb[i % 2]
                fw.dma("sp", y[:, :, :nb], kview(self.yT[:, n0:n0 + nb]), reads=[self.yT], writes=[y])
                fw.dma("sp", x[:, :, :nb], kview(src[:, n0:n0 + nb]), reads=[src], writes=[x])
                for fc in range(8):
                    ps = PS[fc % 4]
                    for k in range(8):
                        fw.op("pe", MM(ps[:, :nb], wo[:, k, fc * 128:(fc + 1) * 128], y[:, k, :nb], k == 0, k == 7), [wo, y],
                              [ps])
                    fw.op("dve", STT(x[:, fc, :nb], ps[:, :nb], self.modT[:, 16 + fc, j:j + 1], x[:, fc, :nb], ALU.mult,
                                     ALU.add), [ps, self.modT, x], [x])
                fw.dma("sp", kview(self.xres[:, n0:n0 + nb]), x[:, :, :nb], reads=[x], writes=[self.xres.sub(n0)])
            fw.barrier()

    def phase_peer(self, l, ntok, last):
        fw = self.fw
        PS = self.PS
        ntile = ntok // 128
        with ExitStack() as pp:
            xn2b = fw.sb("xn2b", [128, 8, NT], BF16, pp)
            with ExitStack() as ph:
                kT = fw.sb("keysT", [128, 16, 128], F32, ph)
                fw.dma("sp", kT[:], self.keysT[l].rearrange("c p f -> p c f"), reads=[self.keysT], writes=[kT])
                xbs = [fw.sb(f"pxb{i}", [128, 8, 128], F32, ph) for i in range(2)]
                prs_ = fw.sb("prs", [128, 128], F32, ph)
                xf = fw.sb("xn2f", [128, 8, 128], F32, ph)
                wqc = [fw.sb(f"wqc{i}", [128, 8, 128], F32, ph) for i in range(2)]
                qTb = fw.sb("qTb", [128, 16, 128], F32, ph)
                sS = fw.sb("sS", [128, 16, 128], F32, ph)
                wk = fw.sb("wk", [128, 256], F32, ph)
                Vt = fw.sb("Vt", [128, 16, 16], F32, ph)
                cand = fw.sb("cand", [128, 8, 256], F32, ph)
                cw_ = fw.sb("cw_", [128, 8, 256], F32, ph)
                ce = fw.sb("ce", [128, 8, 256], F32, ph)
                m8 = fw.sb("m8", [128, 8, 16], F32, ph)
                mx = fw.sb("mx", [128, 8], F32, ph)
                Z = fw.sb("Z", [128, 8], F32, ph)
                sThs = [fw.sb(f"sTh{i}", [128, 16, 128], BF16, ph) for i in range(2)]
                sTls = [fw.sb(f"sTl{i}", [128, 16, 128], BF16, ph) for i in range(2)]
                ths = [fw.sb(f"th{i}", [128, 8], F32, ph) for i in range(2)]
                nbs = [fw.sb(f"nbias{i}", [128, 8], F32, ph) for i in range(2)]
                Wc = [fw.sb(f"Wc{i}", [128, 1024], BF16, ph) for i in range(3)]
                Ah = [fw.sb(f"Ah{i}", [128, 1024], BF16, ph) for i in range(5)]
                Aa = [fw.sb(f"Aa{i}", [128, 128, 128], BF16, ph) for i in range(2)]
                def pre(ti):
                    n0 = ti * 128
                    sTh, sTl, th, nbias = sThs[ti % 2], sTls[ti % 2], ths[ti % 2], nbs[ti % 2]

                    def loadx(t):
                        fw.dma("sp", xbs[t % 2][:], kview(self.xres[:, t * 128:(t + 1) * 128]), reads=[self.xres],
                               writes=[xbs[t % 2]])

                    def loadw(cc):
                        fw.dma("sp", wqc[cc % 2][:], kview(self.wq[l, :, cc * 128:(cc + 1) * 128]), reads=[self.wq],
                               writes=[wqc[cc % 2]])

                    if ti == 0:
                        loadx(0)
                    for cc in range(1):
                        loadw(cc)
                    for _ in self.norm_block_gen((xbs[ti % 2], xf, prs_), self.xres, n0, 128, self.scale2, 24, xn2b, xf, n0,
                                                 skip_dma=True, psb=1):
                        yield
                    if ti + 1 < ntile:
                        loadx(ti + 1)
                    for cc in range(16):
                        w = wqc[cc % 2]
                        if cc + 1 < 16:
                            loadw(cc + 1)
                        ps = PS[cc % 2]
                        for k in range(8):
                            fw.op("pe", MM(ps[:, 0:128], w[:, k, :], xf[:, k, :], k == 0, k == 7), [w, xf], [ps])
                            if k % 2 == 1:
                                yield
                        fw.op("act", ACT(qTb[:, cc, :], ps[:, 0:128], AF.Copy), [ps], [qTb])
                        yield
                    for g in range(4):
                        ps = PS[0]
                        ps2 = PS[1]
                        for c4 in range(4):
                            cc = g * 4 + c4
                            fw.op("pe", MM(ps[:, c4 * 128:(c4 + 1) * 128], qTb[:, cc, :], kT[:, cc, :]), [qTb, kT], [ps])
                        yield
                        for c4 in range(4):
                            cc = g * 4 + c4
                            fw.op("pe", MM(ps2[:, c4 * 128:(c4 + 1) * 128], kT[:, cc, :], qTb[:, cc, :]), [qTb, kT], [ps2])
                        yield
                        fw.op("act", ACT(sS[:, g * 4:(g + 1) * 4, :].rearrange("p c f -> p (c f)"), ps[:, :], AF.Copy), [ps], [sS])
                        fw.op("act", ACT(sTh[:, g * 4:(g + 1) * 4, :].rearrange("p c f -> p (c f)"), ps2[:, :], AF.Copy), [ps2],
                              [sTh])
                        yield
                        fw.op("dve", TT(sTl[:, g * 4:(g + 1) * 4, :].rearrange("p c f -> p (c f)"), ps2[:, :],
                                        sTh[:, g * 4:(g + 1) * 4, :].rearrange("p c f -> p (c f)"), ALU.subtract), [ps2, sTh], [sTl])
                        yield
                    for cc in range(16):
                        fw.op("dve", lambda e, o=Vt[:, cc, 0:8], i_=sS[:, cc, :]: e.max(o, i_), [sS], [Vt])
                        fw.op("dve", lambda e, o=wk[:, 0:128], a=Vt[:, cc, 0:8], i_=sS[:, cc, :]: e.match_replace(o, a, i_, -1e30),
                              [sS, Vt], [wk])
                        fw.op("dve", lambda e, o=Vt[:, cc, 8:16], i_=wk[:, 0:128]: e.max(o, i_), [wk], [Vt])
                        yield
                    V4 = Vt[:].rearrange("p (h t) k -> p h t k", t=2)
                    fw.op("dve", TT(cand[:].rearrange("p h (a b) -> p h a b", b=16),
                                    V4[:, :, 0, :].unsqueeze(3).to_broadcast([128, 8, 16, 16]),
                                    V4[:, :, 1, :].unsqueeze(2).to_broadcast([128, 8, 16, 16]), ALU.add), [Vt], [cand])
                    yield
                    for h in range(8):
                        fw.op("dve", lambda e, o=m8[:, h, 0:8], i_=cand[:, h, :]: e.max(o, i_), [cand], [m8])
                        fw.op("dve", lambda e, o=wk[:, :], a=m8[:, h, 0:8], i_=cand[:, h, :]: e.match_replace(o, a, i_, -1e30),
                              [cand, m8], [wk])
                        fw.op("dve", lambda e, o=m8[:, h, 8:16], i_=wk[:, :]: e.max(o, i_), [wk], [m8])
                        yield
                    fw.op("dve", CP(th[:], m8[:, :, 15]), [m8], [th])
                    fw.op("dve", CP(mx[:], m8[:, :, 0]), [m8], [mx])
                    yield
                    fw.op("dve", TT(cw_[:], cand[:], bc(mx[:], 2, 256), ALU.subtract), [cand, mx], [cw_])
                    fw.op("act", ACT(ce[:], cw_[:], AF.Exp), [cw_], [ce])
                    yield
                    fw.op("dve", TT(cw_[:], cand[:], bc(th[:], 2, 256), ALU.is_ge), [cand, th], [cw_])
                    yield
                    fw.op("dve", TT(ce[:], ce[:], cw_[:], ALU.mult), [ce, cw_], [ce])
                    yield
                    fw.op("dve", lambda e: e.tensor_reduce(Z[:], ce[:], AX.X, ALU.add), [ce], [Z])
                    fw.op("act", ACT(Z[:], Z[:], AF.Ln), [Z], [Z])
                    yield
                    fw.op("dve", TT(nbias[:], mx[:], Z[:], ALU.add), [mx, Z], [nbias])
                    fw.op("dve", TS(nbias[:], nbias[:], -1.0, ALU.mult), [nbias], [nbias])
                    fw.op("dve", TS(th[:], th[:], -SLACK, ALU.add), [th], [th])
                    yield

                r2 = self.identb[:, :].unsqueeze(1).to_broadcast([128, 4, 128])
                DI = fw.sb("DI", [128, 128], BF16, ph)
                fw.op("dve", CP(DI[:, 0:24], self.identb[:, 0:24]), [self.identb], [DI])
                fw.op("dve", TT(DI[:, 24:128], self.identb[:, 24:128], self.identb[:, 0:104], ALU.subtract), [self.identb], [DI])
                for _ in pre(0):
                    pass
                it = 0
                for ti in range(ntile):
                    A = Aa[ti % 2]
                    sTh, sTl, th, nbias = sThs[ti % 2], sTls[ti % 2], ths[ti % 2], nbs[ti % 2]
                    nxt = pre(ti + 1) if ti + 1 < ntile else iter(())
                    for h in range(8):
                        for g in range(16):
                            j = (g + h) % 3
                            it += 1
                            DPt = self.PSD[1 + j]
                            banks = [PS[2 + 2 * j], PS[3 + 2 * j]]
                            for bk in range(2):
                                a0 = g * 8 + bk * 4
                                o = DPt[:, bk * 512:(bk + 1) * 512].rearrange("p (a b) -> p a b", b=128)
                                if g < 3:
                                    r1 = self.identb[:, a0:a0 + 4].unsqueeze(2).to_broadcast([128, 4, 128])
                                    fw.op("pe", MM(o, sTh[:, 2 * h, :], r1, True, False), [sTh, self.identb], [banks[bk]])
                                    fw.op("pe", MM(o, sTl[:, 2 * h, :], r1, False, False), [sTl, self.identb], [banks[bk]])
                                    fw.op("pe", MM(o, sTh[:, 2 * h + 1, :], r2, False, False), [sTh, self.identb], [banks[bk]])
                                    fw.op("pe", MM(o, sTl[:, 2 * h + 1, :], r2, False, True), [sTl, self.identb], [banks[bk]])
                                else:
                                    rd_ = DI[:, a0:a0 + 4].unsqueeze(2).to_broadcast([128, 4, 128])
                                    fw.op("pe", MMS(o, sTh[:, 2 * h, :], rd_), [sTh, DI], [banks[bk]])
                                    fw.op("pe", MMS(o, sTl[:, 2 * h, :], rd_), [sTl, DI], [banks[bk]])
                            W, a_h = Wc[j], Ah[it % 5]
                            fw.op("act", ACT(W[:], DPt[:, 0:1024], AF.Exp, bias=nbias[:, h:h + 1]), banks + [nbias], [W])
                            dst = A[:, g * 8:(g + 1) * 8, :].rearrange("p a b -> p (a b)")
                            if h == 0:
                                fw.op("dve", STT(dst, DPt[:, 0:1024], th[:, h:h + 1], W[:], ALU.is_ge, ALU.mult), banks + [th, W],
                                      [A.sub(g)])
                            else:
                                fw.op("dve", STT(a_h[:], DPt[:, 0:1024], th[:, h:h + 1], W[:], ALU.is_ge, ALU.mult),
                                      banks + [th, W], [a_h])
                                fw.op("dve" if g in (2, 7, 12) else "pool", TT(dst, dst, a_h[:], ALU.add), [A.sub(g), a_h], [A.sub(g)])
                            for _ in range(2):
                                next(nxt, None)
                    for _ in nxt:
                        pass
                    fw.dma("sp", self.Adram[:, :, ti, :].rearrange("c p f -> p c f"),
                           A[:].rearrange("p (c a) b -> p c (a b)", a=4), reads=[A], writes=[self.Adram.sub(ti)])
                fw.barrier()
            if "xn2b_d" in self.dbg:
                d_ = self.scratch("xn2b_d", [1024, NT], BF16)
                fw.dma("sp", kview(d_[:, :]), xn2b[:], reads=[xn2b], writes=[d_])
            if self.stopped(l, "peer1"):
                return
            with ExitStack() as ph0:
              oacc = fw.sb("oacc", [128, 8, NT], F32, ph0)
              with ExitStack() as ph:
                UT = [fw.sb(f"UT{i}", [128, 8, 512], BF16, ph) for i in range(2)]
                Vc = [fw.sb(f"Vc{i}", [128, 4, 1024], BF16, ph) for i in range(2)]
                Ac = [fw.sb(f"Ac{i}", [128, 18, 512], BF16, ph) for i in range(2)]
                Hg = [fw.sb(f"Hg{i}", [128, 512], BF16, ph) for i in range(2)]
                Gg = [fw.sb(f"Gg{i}", [128, 512], BF16, ph) for i in range(2)]
                GT = [fw.sb(f"GT{i}", [128, 4, 512], BF16, ph) for i in range(2)]
                nst = (ntile + 3) // 4

                def load_ec(ec):
                    u, v, a_ = UT[ec % 2], Vc[ec % 2], Ac[ec % 2]
                    fw.dma("pool", u[:], kview(self.uT[l, :, ec * 512:(ec + 1) * 512]), reads=[self.uT], writes=[u])
                    fw.dma("pool", v[:], self.pv[l, ec * 512:(ec + 1) * 512, :].rearrange("(s p) d -> p s d", p=128),
                           reads=[self.pv], writes=[v])
                    fw.dma("sp", a_[:, 0:ntile, :], self.Adram[ec, :, 0:ntile, :], reads=[self.Adram], writes=[a_])

                def emit_h(ec, ti):
                    u = UT[ec % 2]
                    ps = PS[ti % 2]
                    for k in range(8):
                        fw.op("pe", MM(ps[:, :], xn2b[:, k, ti * 128:(ti + 1) * 128], u[:, k, :], k == 0, k == 7), [xn2b, u], [ps])

                def emit_post(ec, ti):
                    a_ = Ac[ec % 2]
                    st, tl = ti // 4, ti % 4
                    gt = GT[st % 2]
                    ps = PS[ti % 2]
                    hg, gg = Hg[ti % 2], Gg[ti % 2]
                    fw.op("act", ACT(hg[:], ps[:, :], AF.Gelu_apprx_tanh), [ps], [hg])
                    fw.op("dve", TT(gg[:], hg[:], a_[:, ti, :], ALU.mult), [hg, a_], [gg])
                    pst = PS[2 + ti % 2]
                    pstb = pst[:].bitcast(BF16)
                    for s_ in range(4):
                        fw.op("pe", TR(pstb[:, s_ * 128:(s_ + 1) * 128], gg[:, s_ * 128:(s_ + 1) * 128], self.identb[:, :]),
                              [gg, self.identb], [pst])
                    fw.op("act", ACT(gt[:, :, tl * 128:(tl + 1) * 128], pstb[:, 0:512].rearrange("p (s n) -> p s n", s=4), AF.Copy),
                          [pst], [gt])

                def emit_out(ec, st):
                    v = Vc[ec % 2]
                    gt = GT[st % 2]
                    nn = (min(st * 4 + 4, ntile) - st * 4) * 128
                    n0 = st * 512
                    for dc in range(8):
                        ps = PS[4 + dc % 4]
                        for s_ in range(4):
                            fw.op("pe", MM(ps[:, :nn], v[:, s_, dc * 128:(dc + 1) * 128], gt[:, s_, :nn], s_ == 0, s_ == 3), [v, gt], [ps])
                        if ec == 0:
                            fw.op("dve", CP(oacc[:, dc, n0:n0 + nn], ps[:, :nn]), [ps], [oacc.sub((dc, st))])
                        else:
                            fw.op("dve", TT(oacc[:, dc, n0:n0 + nn], oacc[:, dc, n0:n0 + nn], ps[:, :nn], ALU.add),
                                  [ps, oacc.sub((dc, st))], [oacc.sub((dc, st))])

                jobs = [(ec, ti) for ec in range(32) for ti in range(ntile)]
                load_ec(0)
                emit_h(*jobs[0])
                for idx, (ec, ti) in enumerate(jobs):
                    if ti == 0 and ec + 1 < 32:
                        load_ec(ec + 1)
                    if idx + 1 < len(jobs):
                        emit_h(*jobs[idx + 1])
                    emit_post(ec, ti)
                    if ti % 4 == 3 or ti == ntile - 1:
                        emit_out(ec, ti // 4)
                fw.barrier()
              with ExitStack() as ph:
                xb = [fw.sb(f"pxo{i}", [128, 8, 512], F32, ph) for i in range(2)]
                for i, (n0, nb) in enumerate(BLK):
                    if n0 >= ntok:
                        continue
                    j = 0 if n0 < TL else 1
                    x = xb[i % 2]
                    fw.dma("sp", x[:, :, :nb], kview(self.xres[:, n0:n0 + nb]), reads=[self.xres], writes=[x])
                    for fc in range(8):
                        fw.op("dve", STT(x[:, fc, :nb], oacc[:, fc, n0:n0 + nb], self.modT[:, 40 + fc, j:j + 1], x[:, fc, :nb],
                                         ALU.mult, ALU.add), [oacc, self.modT, x], [x])
                    dst = self.outT if last else self.xres
                    fw.dma("sp", kview(dst[:, n0:n0 + nb]), x[:, :, :nb], reads=[x], writes=[dst.sub(("o", n0))])
                fw.barrier()


_CONST = {}


def _consts():
    if _CONST:
        return _CONST
    c = {}
    c["c_ident"] = np.eye(128, dtype=np.float32)
    rm = np.zeros((64, 64), np.float32)
    for d in range(64):
        if d % 32 < 16:
            rm[d + 16, d] = -1.0
            rm[d, d + 16] = 1.0
    c["c_rm"] = rm
    s = np.arange(64)[:, None]
    t = np.arange(64)[None, :]
    c["c_mask"] = np.stack([(s <= t), (s >= t)], axis=1).astype(np.float32)
    sel = np.zeros((4, 4, 64), np.float32)
    for h in range(4):
        sel[h, h, :] = 1.0
    c["c_sel"] = sel
    rst = np.ones((4, NT), np.float32)
    rst[:, ::64] = 0.0
    c["c_reset"] = rst
    tt = np.arange(TL)
    row = (tt // 64).astype(np.float32)
    col = (tt % 64).astype(np.float32)
    inv = (np.float32(10000.0) ** (-np.arange(16, dtype=np.float32) / np.float32(16))).astype(np.float32)
    cosT = np.zeros((64, TL), np.float32)
    sinT = np.zeros((64, TL), np.float32)
    for d in range(64):
        ax = d // 32
        f = d % 16
        ang = (row if ax == 0 else col) * inv[f]
        cosT[d] = np.cos(ang.astype(np.float32))
        sinT[d] = np.sin(ang.astype(np.float32))
    c["c_cos"] = cosT
    c["c_sin"] = sinT
    ic = np.zeros((128, 2, LP), np.float32)
    for g, w in enumerate((2, 4, 8, 16)):
        for (T_, off) in ((TL, PL0), (TC, PC0)):
            t_ = np.arange(T_)
            lo = np.clip(t_ - w // 2, 0, T_ - 1)
            hi = np.clip(t_ + (w - w // 2 - 1), 0, T_ - 1)
            cnt = (hi - lo + 1).astype(np.float32)
            ic[(g % 2) * 64:(g % 2) * 64 + 64, g // 2, off:off + T_] = (np.float32(1.0) / cnt)[None, :]
    c["c_ic"] = ic
    _CONST.update(c)
    return _CONST


def _prep_shared(inp):
    f = lambda a: np.ascontiguousarray(np.asarray(a, dtype=np.float32))
    sh = {}
    sh["w_ada"] = f(inp["w_ada"])
    sh["bT"] = f(inp["b_ada"].reshape(2, 48, 128).transpose(0, 2, 1))
    sh["n1g"] = f(inp["norm1_g"].reshape(2, 8, 128).transpose(0, 2, 1))
    sh["n2g"] = f(inp["norm2_g"].reshape(2, 8, 128).transpose(0, 2, 1))
    sh["w_in"] = f(inp["w_in"])
    sh["w_out"] = f(inp["w_out"])
    sh["nag"] = f(np.stack([inp["na_q_g"], inp["na_k_g"]], axis=-1))
    rpb = np.asarray(inp["na_rpb"], np.float32)
    w = np.arange(64)[None, :]
    x = np.arange(64)[:, None]
    c0 = np.clip(w - 8, 0, 48)
    inwin = (x >= c0) & (x < c0 + 16)
    bcol = np.clip(x - w + 15, 0, 30)
    tab = rpb[:, :, :, bcol]
    tab = np.where(inwin[None, None, None], tab, np.float32(-30000.0)).astype(np.float32)
    bt = np.stack([np.concatenate([tab[:, :, s], tab[:, :, s + 1]], axis=2) for s in range(14)], axis=2)
    sh["rpbT"] = f(bt)
    pw = np.asarray(inp["pool_w"], np.float32)
    bd = np.zeros((2, 2, 128, 128), np.float32)
    for g in range(4):
        bd[:, g // 2, (g % 2) * 64:(g % 2) * 64 + 64, (g % 2) * 64:(g % 2) * 64 + 64] = pw[:, g]
    sh["pool_wbd"] = bd
    sh["pool_sc"] = f(inp["pool_scale"].reshape(2, 2, 128).transpose(0, 2, 1))
    cv = np.asarray(inp["ml_conv"], np.float32)
    sh["convw"] = f(cv.reshape(2, 5, 8, 64).transpose(0, 3, 2, 1))
    sh["ml_gb"] = f(inp["ml_gate_b"].reshape(2, 4, 4).transpose(0, 2, 1))
    sh["ml_ng"] = f(inp["ml_norm_g"].reshape(2, 4, 64).transpose(0, 2, 1))
    sh["wq"] = f(inp["peer_wq"])
    pk = np.asarray(inp["peer_keys"], np.float32)
    sh["keysT"] = f(pk.reshape(2, 16, 128, 128).transpose(0, 1, 3, 2))
    sh["uT"] = f(np.asarray(inp["peer_u"], np.float32).transpose(0, 2, 1))
    sh["pv"] = f(inp["peer_v"])
    sh.update(_consts())
    return sh


def _prep_core(inp, b):
    d = {}
    xx = np.concatenate([np.asarray(inp["x"][b], np.float32), np.asarray(inp["ctx"][b], np.float32)], axis=0)
    d["xT"] = np.ascontiguousarray(xx.T)
    cc = np.stack([np.asarray(inp["c"][b], np.float32), np.asarray(inp["c_ctx"], np.float32)], axis=-1)
    d["cT"] = np.ascontiguousarray(cc.reshape(8, 128, 2).transpose(1, 0, 2))
    return d


def kernel(**inputs):
    nc = K().build()
    sh = _prep_shared(inputs)
    in_maps = []
    for b in range(8):
        m = dict(sh)
        m.update(_prep_core(inputs, b))
        in_maps.append(m)
    res = run_bass_kernel_spmd(nc, in_maps, core_ids=list(range(8)))
    out = np.stack([np.ascontiguousarray(np.asarray(r["outT"], np.float32).T) for r in res.results], axis=0)
    return out.astype(np.float32)
```

```python
import math
import numpy as np
from contextlib import ExitStack
import concourse.bass as bass
import concourse.mybir as mybir
from concourse.bass_utils import run_bass_kernel_spmd

F32 = mybir.dt.float32
BF16 = mybir.dt.bfloat16
ALU = mybir.AluOpType
AF = mybir.ActivationFunctionType
AX = mybir.AxisListType

ENGS = ("pe", "act", "dve", "pool", "sp")
NDMASEM = 24
NHW = 16

NT = 2304
TL = 2048
TC = 256
BLK = [(0, 512), (512, 512), (1024, 512), (1536, 512), (2048, 256)]
EPS = 1e-6
LP = 2336
PL0, PC0 = 8, 2072
LC = 2312
CL0, CC0 = 2, 2054
LN8 = math.log(0.125)
SLACK = 3e-4


class Res:
    def __init__(self, name, parent=None):
        self.name = name
        self.parent = parent
        self.kids = {}
        self.w = None
        self.r = {}

    def sub(self, key):
        k = self.kids.get(key)
        if k is None:
            k = Res(f"{self.name}/{key}", self)
            self.kids[key] = k
        return k

    def _related(self):
        out = [self]
        p = self.parent
        while p is not None:
            out.append(p)
            p = p.parent
        st = list(self.kids.values())
        while st:
            k = st.pop()
            out.append(k)
            st.extend(k.kids.values())
        return out


class T:
    def __init__(self, t, name):
        self.t = t
        self.res = Res(name)

    def __getitem__(self, idx):
        return self.t[idx]

    def sub(self, key):
        return self.res.sub(key)


class TV:
    def __init__(self, base, c0, w, name):
        self.base = base
        self.c0 = c0
        self.w = w
        self.res = Res(name)

    def __getitem__(self, idx):
        if not isinstance(idx, tuple):
            idx = (idx, slice(None))
        r, c = idx
        a = self.c0 + (c.start or 0)
        b = self.c0 + (self.w if c.stop is None else c.stop)
        return self.base.t[r, a:b]


class RV:
    def __init__(self, base, p0, n):
        self.base = base
        self.p0 = p0
        self.n = n
        self.res = base.res

    def __getitem__(self, idx):
        r, c = idx
        return self.base.t[self.p0:self.p0 + self.n, c]


class FW:
    def __init__(self, nc, stack):
        self.nc = nc
        self.stack = stack
        self.rec = {e: [] for e in ENGS}
        self.cnt = {e: 0 for e in ENGS}
        self.esem = {e: stack.enter_context(nc.semaphore(f"s_{e}")) for e in ENGS}
        self.dsem = [stack.enter_context(nc.semaphore(f"d_{i}")) for i in range(NDMASEM)]
        self.duse = [0] * NDMASEM
        self.drr = 0
        self.drr_sw = 0
        self.known = {e: {} for e in ENGS}
        self.uid = 0

    def sb(self, name, shape, dt, stack=None):
        self.uid += 1
        t = (stack or self.stack).enter_context(self.nc.sbuf_tensor(f"{name}_{self.uid}", list(shape), dt))
        return T(t, name)

    def ps(self, name, shape, dt=F32):
        t = self.stack.enter_context(self.nc.psum_tensor(name, list(shape), dt))
        return T(t, name)

    def dram(self, name, shape, dt, kind=None):
        if kind is None:
            t = self.nc.dram_tensor(name, list(shape), dt)
        else:
            t = self.nc.dram_tensor(name, list(shape), dt, kind=kind)
        return T(t, name)

    def _need(self, eng, tok, waits):
        if tok is None:
            return
        sem, val = tok
        if eng == "pe" and sem is self.esem["pe"]:
            return
        kn = self.known[eng]
        if kn.get(sem, 0) >= val:
            return
        kn[sem] = val
        waits[sem] = max(waits.get(sem, 0), val)

    def _deps(self, eng, reads, writes, waits):
        for r in reads:
            for x in r._related():
                self._need(eng, x.w, waits)
        for w in writes:
            for x in w._related():
                self._need(eng, x.w, waits)
                for tk in list(x.r.items()):
                    self._need(eng, tk, waits)

    def _commit(self, tok, reads, writes):
        sem, val = tok
        for r in reads:
            if r.r.get(sem, 0) < val:
                r.r[sem] = val
        for w in writes:
            st = list(w.kids.values())
            while st:
                k = st.pop()
                k.w = None
                k.r = {}
                st.extend(k.kids.values())
            w.w = tok
            w.r = {}

    @staticmethod
    def _res(lst):
        return [getattr(x, "res", x) for x in lst]

    def op(self, eng, fn, reads=(), writes=()):
        reads = self._res(reads)
        writes = self._res(writes)
        waits = {}
        self._deps(eng, reads, writes, waits)
        self.cnt[eng] += 1
        tok = (self.esem[eng], self.cnt[eng])
        self.rec[eng].append((list(waits.items()), fn, (self.esem[eng], 1)))
        self._commit(tok, reads, writes)
        return tok

    def dma(self, eng, out, in_, reads=(), writes=(), **kw):
        reads = self._res(reads)
        writes = self._res(writes)
        waits = {}
        self._deps(eng, reads, writes, waits)
        if eng == "pool":
            k = NHW + self.drr_sw
            self.drr_sw = (self.drr_sw + 1) % (NDMASEM - NHW)
        else:
            k = self.drr
            self.drr = (self.drr + 1) % NHW
        sem = self.dsem[k]
        prev = self.duse[k] * 16
        if prev:
            self._need(eng, (sem, prev), waits)
        self.duse[k] += 1
        tok = (sem, self.duse[k] * 16)

        def fn(e, out=out, in_=in_, kw=kw):
            return e.dma_start(out=out, in_=in_, **kw)

        self.rec[eng].append((list(waits.items()), fn, (sem, 16)))
        self._commit(tok, reads, writes)
        return tok

    def _all_tokens(self):
        toks = [(self.esem[e], self.cnt[e]) for e in ENGS if self.cnt[e]]
        toks += [(self.dsem[k], self.duse[k] * 16) for k in range(NDMASEM) if self.duse[k]]
        return toks

    def barrier(self):
        toks = self._all_tokens()
        for e in ENGS:
            waits = {}
            for tk in toks:
                self._need(e, tk, waits)
            if waits:
                self.rec[e].append((list(waits.items()), None, None))

    def finish_wait(self, eng="sp"):
        waits = {}
        for tk in self._all_tokens():
            self._need(eng, tk, waits)
        if waits:
            self.rec[eng].append((list(waits.items()), None, None))

    def emit(self):
        eng_of = {self.esem[e]: e for e in ENGS}
        waited = {e: set() for e in ENGS}
        for e in ENGS:
            for waits, fn, inc in self.rec[e]:
                for sem, val in waits:
                    if sem in eng_of:
                        waited[eng_of[sem]].add(val)
        rank = {e: {v: i + 1 for i, v in enumerate(sorted(waited[e]))} for e in ENGS}
        with self.nc.Block() as block:
            def run(e, lst, me):
                seq = 0
                for waits, fn, inc in lst:
                    for sem, val in waits:
                        if sem in eng_of:
                            e.wait_ge(sem, rank[eng_of[sem]][val])
                        else:
                            e.wait_ge(sem, val)
                    if fn is not None:
                        if inc[0] is self.esem[me]:
                            seq += 1
                            ins = fn(e)
                            if seq in rank[me]:
                                ins.then_inc(inc[0], 1)
                        else:
                            fn(e).then_inc(inc[0], inc[1])

            @block.tensor
            def _(e):
                run(e, self.rec["pe"], "pe")

            @block.scalar
            def _(e):
                run(e, self.rec["act"], "act")

            @block.vector
            def _(e):
                run(e, self.rec["dve"], "dve")

            @block.gpsimd
            def _(e):
                run(e, self.rec["pool"], "pool")

            @block.sync
            def _(e):
                run(e, self.rec["sp"], "sp")


def MM(o, l, r, st=True, sp=True):
    return lambda e: e.matmul(o, l, r, start=st, stop=sp)


def MMS(o, l, r):
    return lambda e: e.matmul(o, l, r, start=False, stop=True, skip_group_check=True)


def TR(o, i, ident):
    return lambda e: e.transpose(o, i, ident)


def ACT(o, i, f, bias=None, scale=None):
    def fn(e):
        kw = {}
        if bias is not None:
            kw["bias"] = bias
        if scale is not None:
            kw["scale"] = scale
        return e.activation(o, i, f, **kw)
    return fn


def TT(o, a, b, op):
    return lambda e: e.tensor_tensor(o, a, b, op)


def TS(o, a, s1, op0, s2=None, op1=None):
    if op1 is None:
        return lambda e: e.tensor_scalar(o, a, s1, None, op0)
    return lambda e: e.tensor_scalar(o, a, s1, s2, op0, op1)


def STT(o, a, s, b, op0, op1):
    return lambda e: e.scalar_tensor_tensor(o, a, s, b, op0, op1)


def CP(o, i):
    return lambda e: e.tensor_copy(o, i)


def RCP(o, i):
    return lambda e: e.reciprocal(o, i)


def MSET(o, v):
    return lambda e: e.memset(o, v)


def bc(ap, axis, n):
    a = ap.unsqueeze(axis)
    shp = list(a.shape)
    shp[axis] = n
    return a.to_broadcast(shp)


def kview(ap2d):
    return ap2d.rearrange("(k p) n -> p k n", p=128)


class K:
    def __init__(self, nlayers=2, dbg=(), stop=None, with_peer=True):
        self.with_peer = with_peer
        self.nlayers = nlayers
        self.dbg = set(dbg)
        self.stop = stop
        self.nc = bass.Bass("TRN2", target_bir_lowering=False)
        self.outs = {}

    def inp(self, name, shape, dt=F32):
        t = self.fw.dram(name, shape, dt, kind="ExternalInput")
        setattr(self, name, t)
        return t

    def scratch(self, name, shape, dt):
        kind = "ExternalOutput" if name in self.dbg else None
        t = self.fw.dram(name, shape, dt, kind=kind)
        if kind:
            self.outs[name] = t
        return t

    def build(self):
        with ExitStack() as st:
            self.st = st
            fw = self.fw = FW(self.nc, st)
            L = 2
            self.inp("xT", [1024, NT])
            self.inp("cT", [128, 8, 2])
            self.inp("w_ada", [L, 1024, 6144])
            self.inp("bT", [L, 128, 48])
            self.inp("n1g", [L, 128, 8])
            self.inp("n2g", [L, 128, 8])
            self.inp("w_in", [L, 1024, 2832])
            self.inp("w_out", [L, 1024, 1024])
            self.inp("nag", [L, 64, 2])
            self.inp("rpbT", [L, 8, 14, 128, 64])
            self.inp("pool_wbd", [L, 2, 128, 128])
            self.inp("pool_sc", [L, 128, 2])
            self.inp("convw", [L, 64, 8, 5])
            self.inp("ml_gb", [L, 4, 4])
            self.inp("ml_ng", [L, 64, 4])
            self.inp("wq", [L, 1024, 2048])
            self.inp("keysT", [L, 16, 128, 128])
            if self.with_peer:
                self.inp("uT", [L, 1024, 16384])
                self.inp("pv", [L, 16384, 1024])
            self.inp("c_ident", [128, 128])
            self.inp("c_rm", [64, 64])
            self.inp("c_mask", [64, 2, 64])
            self.inp("c_sel", [4, 4, 64])
            self.inp("c_reset", [4, NT])
            self.inp("c_cos", [64, TL])
            self.inp("c_sin", [64, TL])
            self.inp("c_ic", [128, 2, LP])
            self.outT = fw.dram("outT", [1024, TL], F32, kind="ExternalOutput")
            self.outs["outT"] = self.outT

            self.xres = self.scratch("xres", [1024, NT], F32)
            self.yT = self.scratch("yT", [1024, NT], BF16)
            self.Adram = self.scratch("Adram", [32, 128, 18, 512], BF16)
            self.gsc = self.scratch("gsc", [2, 3, 4, NT], F32)
            self.gdec = self.scratch("gdec", [2, 4, 36], F32)

            self.PSD = [fw.ps(f"psd{i}", [128, 1024], F32) for i in range(4)]
            self.PS = [TV(self.PSD[i // 2], (i % 2) * 512, 512, f"ps{i}") for i in range(8)]
            self.ident = fw.sb("ident", [128, 128], F32)
            self.identb = fw.sb("identb", [128, 128], BF16)
            self.ones = fw.sb("ones", [128, 128], F32)
            self.onesb = fw.sb("onesb", [128, 128], BF16)
            self.scT = fw.sb("scT", [128, 8, 2], F32)
            self.modT = fw.sb("modT", [128, 48, 2], F32)
            self.scale1 = fw.sb("scale1", [128, 8, 2], F32)
            self.scale2 = fw.sb("scale2", [128, 8, 2], F32)
            fw.dma("sp", self.ident[:], self.c_ident[:], reads=[self.c_ident], writes=[self.ident])
            fw.op("dve", CP(self.identb[:], self.ident[:]), [self.ident], [self.identb])
            fw.op("pool", MSET(self.ones[:], 1.0), [], [self.ones])
            fw.op("pool", MSET(self.onesb[:], 1.0), [], [self.onesb])
            fw.dma("sp", self.scT[:], self.cT[:], reads=[self.cT], writes=[self.scT])
            fw.op("act", ACT(self.scT[:], self.scT[:], AF.Silu), [self.scT], [self.scT])
            self.scTb = fw.sb("scTb", [128, 8, 2], BF16)
            fw.op("dve", CP(self.scTb[:], self.scT[:]), [self.scT], [self.scTb])

            src = self.xT
            for l in range(self.nlayers):
                need_ctx = l < L - 1
                self.layer(l, src, need_ctx)
                src = self.xres
                if self.stop is not None and self.stop[0] == l:
                    break
            fw.finish_wait("sp")
            fw.emit()
        return self.nc

    def stopped(self, l, name):
        return self.stop is not None and self.stop == (l, name)

    def layer(self, l, src, need_ctx):
        fw = self.fw
        last = (l == 1)
        self.phase_mod(l)
        fw.barrier()
        if self.stopped(l, "mod"):
            return
        with ExitStack() as lst:
            xnT = fw.sb("xnT", [128, 8, NT], BF16, lst)
            self.phase_norm(src, self.scale1, self.modT, 0, xnT, None, NT)
            fw.barrier()
            if "xnT_d" in self.dbg:
                d = self.scratch("xnT_d", [1024, NT], BF16)
                fw.dma("sp", kview(d[:, :]), xnT[:], reads=[xnT], writes=[d])
            if self.stopped(l, "norm1"):
                return
            self.phase_na(l, xnT, need_ctx)
            fw.barrier()
            if self.stopped(l, "na"):
                return
            self.phase_pool(l, xnT, need_ctx)
            fw.barrier()
            if self.stopped(l, "pool"):
                return
            self.phase_ml(l, xnT, need_ctx)
            fw.barrier()
            if self.stopped(l, "ml"):
                return
        ntok = NT if need_ctx else TL
        self.phase_out(l, src, ntok)
        fw.barrier()
        if self.stopped(l, "out"):
            return
        self.phase_peer(l, ntok, last)
        fw.barrier()

    def phase_mod(self, l):
        fw = self.fw
        with ExitStack() as ph:
            wa = [fw.sb(f"wa{i}", [128, 8, 512], BF16, ph) for i in range(2)]
            bT = fw.sb("bT_s", [128, 48], F32, ph)
            n1 = fw.sb("n1_s", [128, 8], F32, ph)
            n2 = fw.sb("n2_s", [128, 8], F32, ph)
            fw.dma("sp", bT[:], self.bT[l], reads=[self.bT], writes=[bT])
            fw.dma("sp", n1[:], self.n1g[l], reads=[self.n1g], writes=[n1])
            fw.dma("sp", n2[:], self.n2g[l], reads=[self.n2g], writes=[n2])
            psm = self.PS[0]
            for j in range(12):
                w = wa[j % 2]
                fw.dma("pool", w[:], kview(self.w_ada[l, :, j * 512:(j + 1) * 512]), reads=[self.w_ada], writes=[w])
                for fcl in range(4):
                    fc = j * 4 + fcl
                    for k in range(8):
                        fw.op("pe", MM(psm[:, fc * 2:fc * 2 + 2], w[:, k, fcl * 128:(fcl + 1) * 128], self.scTb[:, k, :],
                                       k == 0, k == 7), [w, self.scTb], [psm])
            fw.op("dve", TT(self.modT[:], psm[:, 0:96].rearrange("p (a b) -> p a b", b=2), bc(bT[:], 2, 2), ALU.add),
                  [psm, bT], [self.modT])
            fw.op("dve", STT(self.scale1[:], self.modT[:, 8:16, :], 1.0, bc(n1[:], 2, 2), ALU.add, ALU.mult),
                  [self.modT, n1], [self.scale1])
            fw.op("dve", STT(self.scale2[:], self.modT[:, 32:40, :], 1.0, bc(n2[:], 2, 2), ALU.add, ALU.mult),
                  [self.modT, n2], [self.scale2])
            if "modT_d" in self.dbg:
                d = self.scratch("modT_d", [128, 96], F32)
                fw.dma("sp", d[:, :], self.modT[:].rearrange("p a b -> p (a b)"), reads=[self.modT], writes=[d])
            fw.barrier()

    def norm_block(self, ph_tiles, src, n0, nb, scale, shoff, out_bf, out_f32, col0):
        for _ in self.norm_block_gen(ph_tiles, src, n0, nb, scale, shoff, out_bf, out_f32, col0):
            pass

    def norm_block_gen(self, ph_tiles, src, n0, nb, scale, shoff, out_bf, out_f32, col0, skip_dma=False, psb=3):
        fw = self.fw
        xb, sq, rs = ph_tiles
        j = 0 if n0 < TL else 1
        ps = self.PS[psb]
        if not skip_dma:
            fw.dma("sp", xb[:, :, :nb], kview(src[:, n0:n0 + nb]), reads=[src], writes=[xb])
        fw.op("act", ACT(sq[:, :, :nb], xb[:, :, :nb], AF.Square), [xb], [sq])
        yield
        for k in range(8):
            onesT = self.onesb if getattr(self, "_norm_bf16", False) else self.ones
            fw.op("pe", MM(ps[:, :nb], onesT[:, :], sq[:, k, :nb], k == 0, k == 7), [onesT, sq], [ps])
        fw.op("act", ACT(rs[:, :nb], ps[:, :nb], AF.Sqrt, bias=EPS, scale=1.0 / 1024.0), [ps], [rs])
        fw.op("dve", RCP(rs[:, :nb], rs[:, :nb]), [rs], [rs])
        yield
        fw.op("dve", TT(xb[:, :, :nb], xb[:, :, :nb], bc(rs[:, :nb], 1, 8), ALU.mult), [xb, rs], [xb])
        yield
        for k in range(8):
            if out_f32 is not None:
                fw.op("act", ACT(out_f32[:, k, :nb], xb[:, k, :nb], AF.Identity,
                                 bias=self.modT[:, shoff + k, j:j + 1], scale=scale[:, k, j:j + 1]),
                      [xb, self.modT, scale], [out_f32])
                fw.op("pool", CP(out_bf[:, k, col0:col0 + nb], out_f32[:, k, :nb]), [out_f32], [out_bf])
                yield
            else:
                fw.op("act", ACT(out_bf[:, k, col0:col0 + nb], xb[:, k, :nb], AF.Identity,
                                 bias=self.modT[:, shoff + k, j:j + 1], scale=scale[:, k, j:j + 1]),
                      [xb, self.modT, scale], [out_bf])

    def phase_norm(self, src, scale, modT, shoff, out_bf, out_f32, ntok):
        fw = self.fw
        with ExitStack() as ph:
            tiles = [(fw.sb("nxb", [128, 8, 512], F32, ph), fw.sb("nsq", [128, 8, 512], BF16, ph),
                      fw.sb("nrs", [128, 512], F32, ph)) for _ in range(2)]
            self._norm_bf16 = True
            for i, (n0, nb) in enumerate(BLK):
                if n0 >= ntok:
                    continue
                self.norm_block(tiles[i % 2], src, n0, nb, scale, shoff, out_bf, out_f32, n0)
            self._norm_bf16 = False
            fw.barrier()

    def proj_fm(self, w, m, xnT, n0, nb, ps):
        fw = self.fw
        for k in range(8):
            fw.op("pe", MM(ps[0:m, :nb], w[:, k, 0:m], xnT[:, k, n0:n0 + nb], k == 0, k == 7), [w, xnT], [ps])

    def rms64(self, src_ps_or_sb, nb, sq, rs, psB, src_res):
        fw = self.fw
        fw.op("act", ACT(sq[0:64, :nb], src_ps_or_sb, AF.Square), [src_res], [sq])
        fw.op("pe", MM(psB[0:64, :nb], self.ones[0:64, 0:64], sq[0:64, :nb]), [self.ones, sq], [psB])
        fw.op("act", ACT(rs[0:64, :nb], psB[0:64, :nb], AF.Sqrt, bias=EPS, scale=1.0 / 64.0), [psB], [rs])
        fw.op("dve", RCP(rs[0:64, :nb], rs[0:64, :nb]), [rs], [rs])

    def phase_na(self, l, xnT, need_ctx):
        fw = self.fw
        PS = self.PS
        with ExitStack() as ph:
            wv = fw.sb("wv", [128, 8, 512], BF16, ph)
            fw.dma("pool", wv[:], kview(self.w_in[l, :, 1024:1536]), reads=[self.w_in], writes=[wv])
            vE = fw.sb("vE", [128, 18, 8, 128], BF16, ph)
            vO = fw.sb("vO", [128, 15, 8, 128], BF16, ph)
            fw.op("pool", MSET(vE[:], 1.0), [], [vE])
            fw.op("pool", MSET(vO[:], 1.0), [], [vO])
            gq = fw.sb("gq", [64, 2], F32, ph)
            fw.dma("sp", gq[:], self.nag[l], reads=[self.nag], writes=[gq])
            fw.op("dve", TS(gq[:, 0:1], gq[:, 0:1], 0.125, ALU.mult), [gq], [gq])
            for ti in range(18 + 15):
                ps = PS[ti % 2]
                t0 = ti * 128 if ti < 18 else 64 + (ti - 18) * 128
                for k in range(8):
                    fw.op("pe", MM(ps[:, :], xnT[:, k, t0:t0 + 128], wv[:, k, :], k == 0, k == 7), [xnT, wv], [ps])
                dst = vE[:, ti, :, 0:64] if ti < 18 else vO[:, ti - 18, :, 0:64]
                fw.op("act", ACT(dst, ps[:, :].rearrange("p (h d) -> p h d", d=64), AF.Copy), [ps], [vE if ti < 18 else vO])
            wqp = [fw.sb(f"wqp{i}", [128, 8, 128], BF16, ph) for i in range(2)]
            wkp = [fw.sb(f"wkp{i}", [128, 8, 128], BF16, ph) for i in range(2)]
            bt = [fw.sb(f"bt{i}", [128, 14, 64], BF16, ph) for i in range(2)]
            qTp = [fw.sb(f"qTp{i}", [128, NT], BF16, ph) for i in range(2)]
            kTp = [fw.sb(f"kTp{i}", [128, NT], BF16, ph) for i in range(2)]
            yh = [fw.sb(f"yh{i}", [64, NT], BF16, ph) for i in range(2)]
            sqs = [fw.sb(f"nsq{i}", [128, 512], BF16, ph) for i in range(2)]
            rss = [fw.sb(f"nrs{i}", [128, 512], F32, ph) for i in range(2)]
            qns = [fw.sb(f"nqn{i}", [128, 512], F32, ph) for i in range(2)]
            nblk = [0]
            bdo = fw.sb("bdones", [128, 128], BF16, ph)
            fw.op("pool", MSET(bdo[:], 0.0), [], [bdo])
            fw.op("pool", MSET(bdo[0:64, 0:64], 1.0), [], [bdo])
            fw.op("pool", MSET(bdo[64:128, 64:128], 1.0), [], [bdo])
            gq2 = fw.sb("gq2", [128, 2], F32, ph)
            fw.dma("sp", gq2[0:64, :], self.nag[l], reads=[self.nag], writes=[gq2])
            fw.dma("sp", gq2[64:128, :], self.nag[l], reads=[self.nag], writes=[gq2])
            fw.op("dve", TS(gq2[:, 0:1], gq2[:, 0:1], 0.125, ALU.mult), [gq2], [gq2])
            ET = [fw.sb(f"ET{i}", [128, 512], BF16, ph) for i in range(4)]
            rd = [fw.sb(f"rd{i}", [128, 256], F32, ph) for i in range(4)]
            for h in range(8):
                b = h % 2
                pair, hh = h // 2, h % 2
                pb = pair % 2
                if hh == 0:
                    fw.dma("pool", wqp[pb][:], kview(self.w_in[l, :, pair * 128:(pair + 1) * 128]), reads=[self.w_in],
                           writes=[wqp[pb]])
                    fw.dma("pool", wkp[pb][:], kview(self.w_in[l, :, 512 + pair * 128:512 + (pair + 1) * 128]),
                           reads=[self.w_in], writes=[wkp[pb]])
                    for (w, dst, gcol, isq) in ((wqp[pb], qTp[pb], 0, True), (wkp[pb], kTp[pb], 1, False)):
                        for (n0, nb) in BLK:
                            if isq and n0 >= TL and not need_ctx:
                                continue
                            i2 = nblk[0] % 2
                            nblk[0] += 1
                            psA, psB = PS[i2], PS[2 + i2]
                            sq, rs, qn = sqs[i2], rss[i2], qns[i2]
                            self.proj_fm(w, 128, xnT, n0, nb, psA)
                            fw.op("act", ACT(sq[:, :nb], psA[:, :nb], AF.Square), [psA], [sq])
                            fw.op("pe", MM(psB[:, :nb], bdo[:, :], sq[:, :nb]), [bdo, sq], [psB])
                            fw.op("act", ACT(rs[:, :nb], psB[:, :nb], AF.Sqrt, bias=EPS, scale=1.0 / 64.0), [psB], [rs])
                            fw.op("dve", RCP(rs[:, :nb], rs[:, :nb]), [rs], [rs])
                            fw.op("dve", TT(qn[:, :nb], psA[:, :nb], rs[:, :nb], ALU.mult), [psA, rs], [qn])
                            fw.op("act", ACT(dst[:, n0:n0 + nb], qn[:, :nb], AF.Identity, scale=gq2[:, gcol:gcol + 1]),
                                  [qn, gq2], [dst])
                fw.dma("pool", bt[b][:], self.rpbT[l, h].rearrange("s p w -> p s w"), reads=[self.rpbT], writes=[bt[b]])
                q, kk, y = RV(qTp[pb], hh * 64, 64), RV(kTp[pb], hh * 64, 64), yh[b]

                def na_qk(r):
                    r0 = min(max(r - 4, 0), 24)
                    delta = r0 - r
                    psS = PS[r % 4]
                    qs = q[:, r * 64:(r + 1) * 64]
                    s0 = delta + 7
                    btv = bt[b][:].rearrange("p (s t) w -> p s t w", t=2)
                    fw.op("pe", MM(psS[:, 0:256].rearrange("p (c w) -> p c w", w=64), self.identb[:, :],
                                   btv[:, s0 // 2:s0 // 2 + 4, s0 % 2, :], True, True), [self.identb, bt[b]], [psS])
                    for c in range(4):
                        k0 = r0 * 64 + c * 128
                        fw.op("pe", MMS(psS[:, c * 64:(c + 1) * 64], kk[:, k0:k0 + 128], qs), [kk, q], [psS])
                    for c in range(2):
                        fw.op("pe", MMS(psS[:, (4 + c) * 64:(5 + c) * 64], kk[:, TL + c * 128:TL + (c + 1) * 128], qs),
                              [kk, q], [psS])

                def na_pv(r):
                    r0 = min(max(r - 4, 0), 24)
                    psS, psO = PS[r % 4], PS[4 + (r % 4)]
                    et, rdd = ET[r % 4], rd[r % 4]
                    fw.op("act", ACT(et[:, 0:384], psS[:, 0:384], AF.Exp), [psS], [et])
                    for c in range(6):
                        if c < 4:
                            k0 = r0 * 64 + c * 128
                            if r0 % 2 == 0:
                                vt, vres = vE[:, k0 // 128, h, :], vE
                            else:
                                vt, vres = vO[:, (k0 - 64) // 128, h, :], vO
                        else:
                            vt, vres = vE[:, 16 + (c - 4), h, :], vE
                        fw.op("pe", MM(psO[:, 0:64], vt, et[:, c * 64:(c + 1) * 64], c == 0, c == 5), [vres, et], [psO])
                    fw.op("dve", RCP(rdd[64:128, 0:64], psO[64:128, 0:64]), [psO], [rdd])
                    fw.op("dve", TT(y[:, r * 64:(r + 1) * 64], psO[0:64, 0:64], rdd[64:128, 0:64], ALU.mult), [psO, rdd], [y])

                na_qk(0)
                for r in range(32):
                    if r + 1 < 32:
                        na_qk(r + 1)
                    na_pv(r)
                if need_ctx:
                    psS, psO = PS[4], PS[6]
                    et, rdd = ET[0], rd[0]
                    for c in range(2):
                        fw.op("pe", MM(psS[:, c * 256:(c + 1) * 256], kk[:, TL + c * 128:TL + (c + 1) * 128], q[:, TL:NT]),
                              [kk, q], [psS])
                    fw.op("act", ACT(et[:, :], psS[:, :], AF.Exp), [psS], [et])
                    for c in range(2):
                        fw.op("pe", MM(psO[0:64, 0:256], vE[:, 16 + c, h, 0:64], et[:, c * 256:(c + 1) * 256],
                                       c == 0, c == 1), [vE, et], [psO])
                    for c in range(2):
                        fw.op("pe", MM(psO[0:64, 256:512], self.onesb[:, 0:64], et[:, c * 256:(c + 1) * 256],
                                       c == 0, c == 1), [self.onesb, et], [psO])
                    fw.op("dve", RCP(rdd[0:64, 0:256], psO[0:64, 256:512]), [psO], [rdd])
                    fw.op("dve", TT(y[:, TL:NT], psO[0:64, 0:256], rdd[0:64, 0:256], ALU.mult), [psO, rdd], [y])
                ncol = NT if need_ctx else TL
                fw.dma("sp", self.yT[h * 64:(h + 1) * 64, 0:ncol], y[:, 0:ncol], reads=[y], writes=[self.yT.sub(("na", h))])
            fw.barrier()

    def phase_pool(self, l, xnT, need_ctx):
        fw = self.fw
        PS = self.PS
        with ExitStack() as ph:
            wp = fw.sb("wp", [128, 8, 256], BF16, ph)
            fw.dma("pool", wp[:], kview(self.w_in[l, :, 1536:1792]), reads=[self.w_in], writes=[wp])
            bd = fw.sb("bd", [128, 2, 128], F32, ph)
            bdb = fw.sb("bdb", [128, 2, 128], BF16, ph)
            fw.dma("sp", bd[:], self.pool_wbd[l].rearrange("c p f -> p c f"), reads=[self.pool_wbd], writes=[bd])
            fw.op("dve", CP(bdb[:], bd[:]), [bd], [bdb])
            psc = fw.sb("psc", [128, 2], F32, ph)
            fw.dma("sp", psc[:], self.pool_sc[l], reads=[self.pool_sc], writes=[psc])
            ic = fw.sb("ic", [128, 2, LP], F32, ph)
            fw.dma("sp", ic[:], self.c_ic[:], reads=[self.c_ic], writes=[ic])
            U = fw.sb("U", [128, 2, LP], F32, ph)
            A2 = fw.sb("A2", [128, 2, LP], F32, ph)
            A4 = fw.sb("A4", [128, 2, LP], F32, ph)
            P = fw.sb("P", [128, 2, LP], F32, ph)
            Pb = fw.sb("Pb", [128, 2, LP], BF16, ph)
            yo = fw.sb("yo", [128, NT], BF16, ph)
            for t in (U, A2, A4, P):
                fw.op("pool", MSET(t[:], 0.0), [], [t])
            for ch in range(2):
                for i, (n0, nb) in enumerate(BLK):
                    ps = PS[i % 2]
                    for k in range(8):
                        fw.op("pe", MM(ps[:, :nb], wp[:, k, ch * 128:(ch + 1) * 128], xnT[:, k, n0:n0 + nb], k == 0, k == 7),
                              [wp, xnT], [ps])
                    off = PL0 + n0 if n0 < TL else PC0
                    fw.op("act", ACT(U[:, ch, off:off + nb], ps[:, :nb], AF.Copy), [ps], [U])
            fw.op("dve", TT(A2[:, :, 1:LP], U[:, :, 0:LP - 1], U[:, :, 1:LP], ALU.add), [U], [A2])
            fw.op("dve", TT(A4[:, :, 1:LP - 1], A2[:, :, 0:LP - 2], A2[:, :, 2:LP], ALU.add), [A2], [A4])
            fw.op("dve", TT(P[0:64, 0, :], A2[0:64, 0, :], ic[0:64, 0, :], ALU.mult), [A2, ic], [P])
            fw.op("dve", TT(P[64:128, 0, :], A4[64:128, 0, :], ic[64:128, 0, :], ALU.mult), [A4, ic], [P])
            A8 = fw.sb("A8", [128, 2, LP], F32, ph)
            A16 = fw.sb("A16", [128, 2, LP], F32, ph)
            fw.op("pool", MSET(A8[:], 0.0), [], [A8])
            fw.op("pool", MSET(A16[:], 0.0), [], [A16])
            fw.op("dve", TT(A8[:, :, 2:LP - 2], A4[:, :, 0:LP - 4], A4[:, :, 4:LP], ALU.add), [A4], [A8])
            fw.op("dve", TT(A16[:, :, 4:LP - 4], A8[:, :, 0:LP - 8], A8[:, :, 8:LP], ALU.add), [A8], [A16])
            fw.op("dve", TT(P[0:64, 1, :], A8[0:64, 1, :], ic[0:64, 1, :], ALU.mult), [A8, ic], [P])
            fw.op("dve", TT(P[64:128, 1, :], A16[64:128, 1, :], ic[64:128, 1, :], ALU.mult), [A16, ic], [P])
            fw.op("dve", TT(Pb[:], P[:], U[:], ALU.subtract), [P, U], [Pb])
            for ch in range(2):
                for i, (n0, nb) in enumerate(BLK):
                    if n0 >= TL and not need_ctx:
                        continue
                    ps = PS[2 + i % 2]
                    off = PL0 + n0 if n0 < TL else PC0
                    fw.op("pe", MM(ps[:, :nb], bdb[:, ch, :], Pb[:, ch, off:off + nb]), [bdb, Pb], [ps])
                    fw.op("act", ACT(yo[:, n0:n0 + nb], ps[:, :nb], AF.Identity, scale=psc[:, ch:ch + 1]), [ps, psc], [yo])
                ncol = NT if need_ctx else TL
                fw.dma("sp", self.yT[512 + ch * 128:512 + (ch + 1) * 128, 0:ncol], yo[:, 0:ncol], reads=[yo],
                       writes=[self.yT.sub(("pool", ch))])
            fw.barrier()

    def phase_ml(self, l, xnT, need_ctx):
        fw = self.fw
        PS = self.PS
        gsc = self.gsc
        with ExitStack() as ph:
            wg = fw.sb("wg", [128, 8, 16], BF16, ph)
            fw.dma("pool", wg[:], kview(self.w_in[l, :, 2816:2832]), reads=[self.w_in], writes=[wg])
            gb = fw.sb("gb", [4, 4], F32, ph)
            fw.dma("sp", gb[:], self.ml_gb[l], reads=[self.ml_gb], writes=[gb])
            rst = fw.sb("rst", [4, NT], F32, ph)
            fw.dma("sp", rst[:], self.c_reset[:], reads=[self.c_reset], writes=[rst])
            G = [fw.sb(f"G{ty}", [4, NT], F32, ph) for ty in range(4)]
            tmp4 = fw.sb("tmp4", [4, NT], F32, ph)
            for ty in range(4):
                for i, (n0, nb) in enumerate(BLK):
                    ps = PS[2 + i % 2]
                    for k in range(8):
                        fw.op("pe", MM(ps[0:4, :nb], wg[:, k, ty * 4:(ty + 1) * 4], xnT[:, k, n0:n0 + nb], k == 0, k == 7),
                              [wg, xnT], [ps])
                    fw.op("act", ACT(G[ty][:, n0:n0 + nb], ps[0:4, :nb], AF.Identity, bias=gb[:, ty:ty + 1]), [ps, gb],
                          [G[ty]])
            for ty in (1, 3):
                fw.op("act", ACT(tmp4[:], G[ty][:], AF.Exp, scale=-1.0), [G[ty]], [tmp4])
                fw.op("act", ACT(tmp4[:], tmp4[:], AF.Ln, bias=1.0), [tmp4], [tmp4])
                fw.op("dve", TS(G[ty][:], tmp4[:], -1.0, ALU.mult), [tmp4], [G[ty]])
            Bc = [fw.sb(f"Bc{d}", [4, NT], F32, ph) for d in range(2)]
            fw.op("dve", lambda e: e.tensor_tensor_scan(Bc[0][:], rst[:], G[1][:], 0.0, ALU.mult, ALU.add), [rst, G[1]],
                  [Bc[0]])
            fw.op("dve", lambda e: e.tensor_tensor_scan(tmp4[:], rst[:], G[3][:], 0.0, ALU.mult, ALU.add), [rst, G[3]],
                  [tmp4])
            v3 = lambda t: t[:].rearrange("p (c s) -> p c s", s=64)
            fw.op("dve", TT(v3(Bc[1]), v3(tmp4)[:, :, 63:64].to_broadcast([4, 36, 64]), v3(tmp4), ALU.subtract), [tmp4],
                  [Bc[1]])
            fw.op("dve", TT(Bc[1][:], Bc[1][:], G[3][:], ALU.add), [Bc[1], G[3]], [Bc[1]])
            eb = fw.sb("EB", [4, NT], F32, ph)
            ak = fw.sb("AK", [4, NT], F32, ph)
            ks = fw.sb("KS", [4, NT], F32, ph)
            dec = fw.sb("DEC", [4, 36], F32, ph)
            for d in range(2):
                gi = G[0] if d == 0 else G[2]
                bl = v3(Bc[d])[:, :, 63:64] if d == 0 else v3(Bc[d])[:, :, 0:1]
                fw.op("act", ACT(eb[:], Bc[d][:], AF.Exp), [Bc[d]], [eb])
                fw.op("dve", TT(ak[:], gi[:], Bc[d][:], ALU.subtract), [gi, Bc[d]], [ak])
                fw.op("dve", TT(v3(ks), v3(ak), bl.to_broadcast([4, 36, 64]), ALU.add), [ak, Bc[d]], [ks])
                fw.op("act", ACT(ak[:], ak[:], AF.Exp, bias=LN8), [ak], [ak])
                fw.op("act", ACT(ks[:], ks[:], AF.Exp, bias=LN8), [ks], [ks])
                fw.op("act", ACT(dec[:].unsqueeze(2), bl, AF.Exp), [Bc[d]], [dec])
                for kind, t in enumerate((eb, ak, ks)):
                    fw.dma("sp", gsc[d, kind], t[:], reads=[t], writes=[gsc.sub((d, kind))])
                fw.dma("sp", self.gdec[d], dec[:], reads=[dec], writes=[self.gdec.sub(d)])
            fw.barrier()
        with ExitStack() as ph:
            wmv = fw.sb("wmv", [128, 8, 256], BF16, ph)
            fw.dma("pool", wmv[:], kview(self.w_in[l, :, 2304:2560]), reads=[self.w_in], writes=[wmv])
            ng = fw.sb("ng", [64, 4], F32, ph)
            fw.dma("sp", ng[:], self.ml_ng[l], reads=[self.ml_ng], writes=[ng])
            cw = fw.sb("cw", [64, 8, 5], F32, ph)
            fw.dma("sp", cw[:], self.convw[l], reads=[self.convw], writes=[cw])
            rm = fw.sb("rm", [64, 64], F32, ph)
            fw.dma("sp", rm[:], self.c_rm[:], reads=[self.c_rm], writes=[rm])
            mask = fw.sb("mask", [64, 2, 64], F32, ph)
            fw.dma("sp", mask[:], self.c_mask[:], reads=[self.c_mask], writes=[mask])
            cos = fw.sb("cos", [64, TL], F32, ph)
            sin = fw.sb("sin", [64, TL], F32, ph)
            fw.dma("sp", cos[:], self.c_cos[:], reads=[self.c_cos], writes=[cos])
            fw.dma("sp", sin[:], self.c_sin[:], reads=[self.c_sin], writes=[sin])
            vt = fw.sb("vth", [64, 36, 64], F32, ph)
            U_ = fw.sb("Ucv", [64, LC], F32, ph)
            fw.op("pool", MSET(U_[:], 0.0), [], [U_])
            acc = fw.sb("acc", [64, LC], F32, ph)
            qc = fw.sb("qc", [64, NT], F32, ph)
            kc = fw.sb("kc", [64, NT], F32, ph)
            t1 = fw.sb("t1", [64, 512], F32, ph)
            t2 = fw.sb("t2", [64, 512], F32, ph)
            qp = fw.sb("qp", [64, NT], F32, ph)
            kp = fw.sb("kp", [64, NT], F32, ph)
            kq = fw.sb("kq", [64, NT], F32, ph)
            ktok = fw.sb("ktok", [64, 36, 64], F32, ph)
            decb = fw.sb("decb", [64, 36], F32, ph)
            HT = fw.sb("HT", [64, NT], F32, ph)
            Ca = [fw.sb(f"Ca{i}", [64, 128], F32, ph) for i in range(2)]
            PT = [fw.sb(f"PT{i}", [64, 64], F32, ph) for i in range(6)]
            dn = [fw.sb(f"dn{i}", [64, 64], F32, ph) for i in range(6)]
            hb = [fw.sb(f"hb{i}", [64, 64], F32, ph) for i in range(6)]
            wmh = [fw.sb(f"wmh{i}", [128, 8, 64], BF16, ph) for i in range(3)]
            yo = fw.sb("yo", [64, NT], BF16, ph)
            sqh = fw.sb("sqh", [64, 512], F32, ph)
            rsh = fw.sb("rsh", [64, 512], F32, ph)
            sg = fw.sb("sgb", [64, 512], F32, ph)
            ncol = NT if need_ctx else TL
            for h in range(4):
                for i, c0 in enumerate((1792, 2048, 2560)):
                    fw.dma("pool", wmh[i][:], kview(self.w_in[l, :, c0 + h * 64:c0 + (h + 1) * 64]), reads=[self.w_in],
                           writes=[wmh[i]])
                for c in range(36):
                    ps = PS[c % 2]
                    for k in range(8):
                        fw.op("pe", MM(ps[0:64, 0:64], xnT[:, k, c * 64:(c + 1) * 64], wmv[:, k, h * 64:(h + 1) * 64],
                                       k == 0, k == 7), [xnT, wmv], [ps])
                    fw.op("act", ACT(vt[:, c, :], ps[0:64, 0:64], AF.Copy), [ps], [vt])
                for (w, dst, ci) in ((wmh[0], qc, h), (wmh[1], kc, 4 + h)):
                    for i, (n0, nb) in enumerate(BLK):
                        off = CL0 + n0 if n0 < TL else CC0
                        ps = PS[2 + i % 2]
                        self.proj_fm(w, 64, xnT, n0, nb, ps)
                        fw.op("act", ACT(U_[:, off:off + nb], ps[0:64, :nb], AF.Copy), [ps], [U_])
                    fw.op("dve", TS(acc[:, 0:LC - 4], U_[:, 0:LC - 4], cw[:, ci, 0:1], ALU.mult), [U_, cw], [acc])
                    for j in range(1, 5):
                        fw.op("dve", STT(acc[:, 0:LC - 4], U_[:, j:LC - 4 + j], cw[:, ci, j:j + 1], acc[:, 0:LC - 4],
                                         ALU.mult, ALU.add), [U_, cw, acc], [acc])
                    fw.op("act", ACT(dst[:, 0:TL], acc[:, 0:TL], AF.Silu), [acc], [dst])
                    fw.op("act", ACT(dst[:, TL:NT], acc[:, CC0 - 2:CC0 - 2 + TC], AF.Silu), [acc], [dst])
                    for i in range(4):
                        n0 = i * 512
                        ps = PS[4 + i % 2]
                        fw.op("pe", MM(ps[0:64, :], rm[:, :], dst[:, n0:n0 + 512]), [rm, dst], [ps])
                        fw.op("dve", TT(t1[:], dst[:, n0:n0 + 512], cos[:, n0:n0 + 512], ALU.mult), [dst, cos], [t1])
                        fw.op("dve", TT(t2[:], ps[0:64, :], sin[:, n0:n0 + 512], ALU.mult), [ps, sin], [t2])
                        fw.op("pool", TT(dst[:, n0:n0 + 512], t1[:], t2[:], ALU.add), [t1, t2], [dst])
                if "mlq_d" in self.dbg and h == 0:
                    d_ = self.scratch("mlq_d", [64, 2, NT], F32)
                    for i, t in enumerate([qc, kc]):
                        fw.dma("sp", d_[:, i, :], t[:], reads=[t], writes=[d_.sub(i)])
                order = [list(range(32, 36)) + list(range(0, 32)), [35, 34, 33, 32] + list(range(31, -1, -1))]
                for d in range(2):
                    for kind, dstT, base in ((0, qp, qc), (1, kp, kc), (2, kq, kc)):
                        fw.dma("sp", dstT[:], gsc[d, kind, h:h + 1, :].to_broadcast([64, NT]), reads=[gsc], writes=[dstT])
                        fw.op("dve" if kind < 2 else "pool", TT(dstT[:], dstT[:], base[:], ALU.mult), [dstT, base], [dstT])
                    fw.dma("sp", decb[:], self.gdec[d, h:h + 1, :].to_broadcast([64, 36]), reads=[self.gdec], writes=[decb])
                    for g8 in range(5):
                        pst = PS[6 + g8 % 2]
                        ncs = 8 if g8 < 4 else 4
                        for cc in range(ncs):
                            c = g8 * 8 + cc
                            fw.op("pe", TR(pst[0:64, cc * 64:(cc + 1) * 64], kq[:, c * 64:(c + 1) * 64], self.ident[0:64, 0:64]),
                                  [kq, self.ident], [pst])
                        fw.op("act", ACT(ktok[:, g8 * 8:g8 * 8 + ncs, :].rearrange("p c s -> p (c s)"), pst[0:64, 0:ncs * 64],
                                         AF.Copy), [pst], [ktok])
                    fw.op("pool", MSET(Ca[0][:], 0.0), [], [Ca[0]])
                    def ml_bank(step):
                        bank = PS[step % 6]
                        psS = TV(bank.base, bank.c0, 64, "x")
                        psN = TV(bank.base, bank.c0 + 64, 128, "x")
                        psC = TV(bank.base, bank.c0 + 192, 128, "x")
                        psS.res = psN.res = psC.res = bank.res
                        return psS, psN, psC

                    def ml_a(step):
                        c = order[d][step]
                        t0 = c * 64
                        psS, psN, psC = ml_bank(step)
                        if need_ctx or c < 32:
                            fw.op("pe", MM(psS[0:64, 0:64], kp[:, t0:t0 + 64], qp[:, t0:t0 + 64]), [kp, qp], [psS])
                            fw.op("dve", TT(PT[step % 6][:], psS[0:64, 0:64], mask[:, d, :], ALU.mult), [psS, mask], [PT[step % 6]])

                    def ml_b(step):
                        c = order[d][step]
                        t0 = c * 64
                        psS, psN, psC = ml_bank(step)
                        cur, nxt = Ca[step % 2], Ca[(step + 1) % 2]
                        pt, dnn, hbb = PT[step % 6], dn[step % 6], hb[step % 6]
                        vv = vt[:, c, :]
                        qch = qp[:, t0:t0 + 64]
                        if need_ctx or c < 32:
                            fw.op("pe", MM(psN[0:64, 0:64], vv, pt[:], True, False), [vt, pt], [psN])
                            fw.op("pe", MM(psN[0:64, 0:64], cur[:, 0:64], qch, False, True), [cur, qp], [psN])
                            fw.op("pe", MM(psN[0:64, 64:128], self.ones[0:64, 0:64], pt[:], True, False), [self.ones, pt],
                                  [psN])
                            fw.op("pe", MM(psN[0:64, 64:128], cur[:, 64:128], qch, False, True), [cur, qp], [psN])
                        if step < 35:
                            fw.op("pe", MM(psC[0:64, 0:64], ktok[:, c, :], vv), [ktok, vt], [psC])
                            fw.op("pe", MM(psC[0:64, 64:128], ktok[:, c, :], self.ones[0:64, 0:64]), [ktok, self.ones], [psC])
                            fw.op("dve", STT(nxt[:], cur[:], decb[:, c:c + 1], psC[0:64, 0:128], ALU.mult, ALU.add),
                                  [cur, decb, psC], [nxt])
                        if need_ctx or c < 32:
                            fw.op("dve", TS(dnn[:], psN[0:64, 64:128], 1.0, ALU.max), [psN], [dnn])
                            fw.op("dve", STT(dnn[:], psN[0:64, 64:128], -1.0, dnn[:], ALU.mult, ALU.max), [psN, dnn], [dnn])
                            fw.op("dve", RCP(dnn[:], dnn[:]), [dnn], [dnn])
                            if d == 0:
                                fw.op("dve", TT(HT[:, t0:t0 + 64], psN[0:64, 0:64], dnn[:], ALU.mult), [psN, dnn], [HT])
                            else:
                                fw.op("dve", TT(hbb[:], psN[0:64, 0:64], dnn[:], ALU.mult), [psN, dnn], [hbb])
                                fw.op("pool", TT(HT[:, t0:t0 + 64], HT[:, t0:t0 + 64], hbb[:], ALU.add), [HT, hbb], [HT])

                    ml_a(0)
                    for step in range(36):
                        if step + 1 < 36:
                            ml_a(step + 1)
                        ml_b(step)
                if "mlh_d" in self.dbg and h == 0:
                    d_ = self.scratch("mlh_d", [64, NT], F32)
                    fw.dma("sp", d_[:, :], HT[:], reads=[HT], writes=[d_])
                for i, (n0, nb) in enumerate(BLK):
                    if n0 >= ncol:
                        continue
                    ps = PS[6]
                    self.proj_fm(wmh[2], 64, xnT, n0, nb, ps)
                    fw.op("act", ACT(sg[:, :nb], ps[0:64, :nb], AF.Sigmoid), [ps], [sg])
                    psB = PS[7]
                    self.rms64(HT[:, n0:n0 + nb], nb, sqh, rsh, psB, HT)
                    fw.op("dve", TT(t1[:, :nb], HT[:, n0:n0 + nb], rsh[0:64, :nb], ALU.mult), [HT, rsh], [t1])
                    fw.op("dve", STT(yo[:, n0:n0 + nb], t1[:, :nb], ng[:, h:h + 1], sg[:, :nb], ALU.mult, ALU.mult),
                          [t1, ng, sg], [yo])
                fw.dma("sp", self.yT[768 + h * 64:768 + (h + 1) * 64, 0:ncol], yo[:, 0:ncol], reads=[yo],
                       writes=[self.yT.sub(("ml", h))])
            fw.barrier()

    def phase_out(self, l, src, ntok):
        fw = self.fw
        PS = self.PS
        with ExitStack() as ph:
            wo = fw.sb("wo", [128, 8, 1024], BF16, ph)
            fw.dma("pool", wo[:], kview(self.w_out[l]), reads=[self.w_out], writes=[wo])
            yb = [fw.sb(f"yb{i}", [128, 8, 512], BF16, ph) for i in range(2)]
            xb = [fw.sb(f"xb{i}", [128, 8, 512], F32, ph) for i in range(2)]
            for i, (n0, nb) in enumerate(BLK):
                if n0 >= ntok:
                    continue
                j = 0 if n0 < TL else 1
                y, x = yb[i % 2], xb[i % 2]
                fw.dma("sp", y[:, :, :nb], kview(self.yT[:, n0:n0 + nb]), reads=[self.yT], writes=[y])
                fw.dma("sp", x[:, :, :nb], kview(src[:, n0:n0 + nb]), reads=[src], writes=[x])
                for fc in range(8):
                    ps = PS[fc % 4]
                    for k in range(8):
                        fw.op("pe", MM(ps[:, :nb], wo[:, k, fc * 128:(fc + 1) * 128], y[:, k, :nb], k == 0, k == 7), [wo, y],
                              [ps])
                    fw.op("dve", STT(x[:, fc, :nb], ps[:, :nb], self.modT[:, 16 + fc, j:j + 1], x[:, fc, :nb], ALU.mult,
                                     ALU.add), [ps, self.modT, x], [x])
                fw.dma("sp", kview(self.xres[:, n0:n0 + nb]), x[:, :, :nb], reads=[x], writes=[self.xres.sub(n0)])
            fw.barrier()

    def phase_peer(self, l, ntok, last):
        fw = self.fw
        PS = self.PS
        ntile = ntok // 128
        with ExitStack() as pp:
            xn2b = fw.sb("xn2b", [128, 8, NT], BF16, pp)
            with ExitStack() as ph:
                kT = fw.sb("keysT", [128, 16, 128], F32, ph)
                fw.dma("sp", kT[:], self.keysT[l].rearrange("c p f -> p c f"), reads=[self.keysT], writes=[kT])
                xbs = [fw.sb(f"pxb{i}", [128, 8, 128], F32, ph) for i in range(2)]
                prs_ = fw.sb("prs", [128, 128], F32, ph)
                xf = fw.sb("xn2f", [128, 8, 128], F32, ph)
                wqc = [fw.sb(f"wqc{i}", [128, 8, 128], F32, ph) for i in range(2)]
                qTb = fw.sb("qTb", [128, 16, 128], F32, ph)
                sS = fw.sb("sS", [128, 16, 128], F32, ph)
                wk = fw.sb("wk", [128, 256], F32, ph)
                Vt = fw.sb("Vt", [128, 16, 16], F32, ph)
                cand = fw.sb("cand", [128, 8, 256], F32, ph)
                cw_ = fw.sb("cw_", [128, 8, 256], F32, ph)
                ce = fw.sb("ce", [128, 8, 256], F32, ph)
                m8 = fw.sb("m8", [128, 8, 16], F32, ph)
                mx = fw.sb("mx", [128, 8], F32, ph)
                Z = fw.sb("Z", [128, 8], F32, ph)
                sThs = [fw.sb(f"sTh{i}", [128, 16, 128], BF16, ph) for i in range(2)]
                sTls = [fw.sb(f"sTl{i}", [128, 16, 128], BF16, ph) for i in range(2)]
                ths = [fw.sb(f"th{i}", [128, 8], F32, ph) for i in range(2)]
                nbs = [fw.sb(f"nbias{i}", [128, 8], F32, ph) for i in range(2)]
                Wc = [fw.sb(f"Wc{i}", [128, 1024], BF16, ph) for i in range(3)]
                Ah = [fw.sb(f"Ah{i}", [128, 1024], BF16, ph) for i in range(5)]
                Aa = [fw.sb(f"Aa{i}", [128, 128, 128], BF16, ph) for i in range(2)]
                def pre(ti):
                    n0 = ti * 128
                    sTh, sTl, th, nbias = sThs[ti % 2], sTls[ti % 2], ths[ti % 2], nbs[ti % 2]

                    def loadx(t):
                        fw.dma("sp", xbs[t % 2][:], kview(self.xres[:, t * 128:(t + 1) * 128]), reads=[self.xres],
                               writes=[xbs[t % 2]])

                    def loadw(cc):
                        fw.dma("sp", wqc[cc % 2][:], kview(self.wq[l, :, cc * 128:(cc + 1) * 128]), reads=[self.wq],
                               writes=[wqc[cc % 2]])

                    if ti == 0:
                        loadx(0)
                    for cc in range(1):
                        loadw(cc)
                    for _ in self.norm_block_gen((xbs[ti % 2], xf, prs_), self.xres, n0, 128, self.scale2, 24, xn2b, xf, n0,
                                                 skip_dma=True, psb=1):
                        yield
                    if ti + 1 < ntile:
                        loadx(ti + 1)
                    for cc in range(16):
                        w = wqc[cc % 2]
                        if cc + 1 < 16:
                            loadw(cc + 1)
                        ps = PS[cc % 2]
                        for k in range(8):
                            fw.op("pe", MM(ps[:, 0:128], w[:, k, :], xf[:, k, :], k == 0, k == 7), [w, xf], [ps])
                            if k % 2 == 1:
                                yield
                        fw.op("act", ACT(qTb[:, cc, :], ps[:, 0:128], AF.Copy), [ps], [qTb])
                        yield
                    for g in range(4):
                        ps = PS[0]
                        ps2 = PS[1]
                        for c4 in range(4):
                            cc = g * 4 + c4
                            fw.op("pe", MM(ps[:, c4 * 128:(c4 + 1) * 128], qTb[:, cc, :], kT[:, cc, :]), [qTb, kT], [ps])
                        yield
                        for c4 in range(4):
                            cc = g * 4 + c4
                            fw.op("pe", MM(ps2[:, c4 * 128:(c4 + 1) * 128], kT[:, cc, :], qTb[:, cc, :]), [qTb, kT], [ps2])
                        yield
                        fw.op("act", ACT(sS[:, g * 4:(g + 1) * 4, :].rearrange("p c f -> p (c f)"), ps[:, :], AF.Copy), [ps], [sS])
                        fw.op("act", ACT(sTh[:, g * 4:(g + 1) * 4, :].rearrange("p c f -> p (c f)"), ps2[:, :], AF.Copy), [ps2],
                              [sTh])
                        yield
                        fw.op("dve", TT(sTl[:, g * 4:(g + 1) * 4, :].rearrange("p c f -> p (c f)"), ps2[:, :],
                                        sTh[:, g * 4:(g + 1) * 4, :].rearrange("p c f -> p (c f)"), ALU.subtract), [ps2, sTh], [sTl])
                        yield
                    for cc in range(16):
                        fw.op("dve", lambda e, o=Vt[:, cc, 0:8], i_=sS[:, cc, :]: e.max(o, i_), [sS], [Vt])
                        fw.op("dve", lambda e, o=wk[:, 0:128], a=Vt[:, cc, 0:8], i_=sS[:, cc, :]: e.match_replace(o, a, i_, -1e30),
                              [sS, Vt], [wk])
                        fw.op("dve", lambda e, o=Vt[:, cc, 8:16], i_=wk[:, 0:128]: e.max(o, i_), [wk], [Vt])
                        yield
                    V4 = Vt[:].rearrange("p (h t) k -> p h t k", t=2)
                    fw.op("dve", TT(cand[:].rearrange("p h (a b) -> p h a b", b=16),
                                    V4[:, :, 0, :].unsqueeze(3).to_broadcast([128, 8, 16, 16]),
                                    V4[:, :, 1, :].unsqueeze(2).to_broadcast([128, 8, 16, 16]), ALU.add), [Vt], [cand])
                    yield
                    for h in range(8):
                        fw.op("dve", lambda e, o=m8[:, h, 0:8], i_=cand[:, h, :]: e.max(o, i_), [cand], [m8])
                        fw.op("dve", lambda e, o=wk[:, :], a=m8[:, h, 0:8], i_=cand[:, h, :]: e.match_replace(o, a, i_, -1e30),
                              [cand, m8], [wk])
                        fw.op("dve", lambda e, o=m8[:, h, 8:16], i_=wk[:, :]: e.max(o, i_), [wk], [m8])
                        yield
                    fw.op("dve", CP(th[:], m8[:, :, 15]), [m8], [th])
                    fw.op("dve", CP(mx[:], m8[:, :, 0]), [m8], [mx])
                    yield
                    fw.op("dve", TT(cw_[:], cand[:], bc(mx[:], 2, 256), ALU.subtract), [cand, mx], [cw_])
                    fw.op("act", ACT(ce[:], cw_[:], AF.Exp), [cw_], [ce])
                    yield
                    fw.op("dve", TT(cw_[:], cand[:], bc(th[:], 2, 256), ALU.is_ge), [cand, th], [cw_])
                    yield
                    fw.op("dve", TT(ce[:], ce[:], cw_[:], ALU.mult), [ce, cw_], [ce])
                    yield
                    fw.op("dve", lambda e: e.tensor_reduce(Z[:], ce[:], AX.X, ALU.add), [ce], [Z])
                    fw.op("act", ACT(Z[:], Z[:], AF.Ln), [Z], [Z])
                    yield
                    fw.op("dve", TT(nbias[:], mx[:], Z[:], ALU.add), [mx, Z], [nbias])
                    fw.op("dve", TS(nbias[:], nbias[:], -1.0, ALU.mult), [nbias], [nbias])
                    fw.op("dve", TS(th[:], th[:], -SLACK, ALU.add), [th], [th])
                    yield

                r2 = self.identb[:, :].unsqueeze(1).to_broadcast([128, 4, 128])
                DI = fw.sb("DI", [128, 128], BF16, ph)
                fw.op("dve", CP(DI[:, 0:24], self.identb[:, 0:24]), [self.identb], [DI])
                fw.op("dve", TT(DI[:, 24:128], self.identb[:, 24:128], self.identb[:, 0:104], ALU.subtract), [self.identb], [DI])
                for _ in pre(0):
                    pass
                it = 0
                for ti in range(ntile):
                    A = Aa[ti % 2]
                    sTh, sTl, th, nbias = sThs[ti % 2], sTls[ti % 2], ths[ti % 2], nbs[ti % 2]
                    nxt = pre(ti + 1) if ti + 1 < ntile else iter(())
                    for h in range(8):
                        for g in range(16):
                            j = (g + h) % 3
                            it += 1
                            DPt = self.PSD[1 + j]
                            banks = [PS[2 + 2 * j], PS[3 + 2 * j]]
                            for bk in range(2):
                                a0 = g * 8 + bk * 4
                                o = DPt[:, bk * 512:(bk + 1) * 512].rearrange("p (a b) -> p a b", b=128)
                                if g < 3:
                                    r1 = self.identb[:, a0:a0 + 4].unsqueeze(2).to_broadcast([128, 4, 128])
                                    fw.op("pe", MM(o, sTh[:, 2 * h, :], r1, True, False), [sTh, self.identb], [banks[bk]])
                                    fw.op("pe", MM(o, sTl[:, 2 * h, :], r1, False, False), [sTl, self.identb], [banks[bk]])
                                    fw.op("pe", MM(o, sTh[:, 2 * h + 1, :], r2, False, False), [sTh, self.identb], [banks[bk]])
                                    fw.op("pe", MM(o, sTl[:, 2 * h + 1, :], r2, False, True), [sTl, self.identb], [banks[bk]])
                                else:
                                    rd_ = DI[:, a0:a0 + 4].unsqueeze(2).to_broadcast([128, 4, 128])
                                    fw.op("pe", MMS(o, sTh[:, 2 * h, :], rd_), [sTh, DI], [banks[bk]])
                                    fw.op("pe", MMS(o, sTl[:, 2 * h, :], rd_), [sTl, DI], [banks[bk]])
                            W, a_h = Wc[j], Ah[it % 5]
                            fw.op("act", ACT(W[:], DPt[:, 0:1024], AF.Exp, bias=nbias[:, h:h + 1]), banks + [nbias], [W])
                            dst = A[:, g * 8:(g + 1) * 8, :].rearrange("p a b -> p (a b)")
                            if h == 0:
                                fw.op("dve", STT(dst, DPt[:, 0:1024], th[:, h:h + 1], W[:], ALU.is_ge, ALU.mult), banks + [th, W],
                                      [A.sub(g)])
                            else:
                                fw.op("dve", STT(a_h[:], DPt[:, 0:1024], th[:, h:h + 1], W[:], ALU.is_ge, ALU.mult),
                                      banks + [th, W], [a_h])
                                fw.op("dve" if g in (2, 7, 12) else "pool", TT(dst, dst, a_h[:], ALU.add), [A.sub(g), a_h], [A.sub(g)])
                            for _ in range(2):
                                next(nxt, None)
                    for _ in nxt:
                        pass
                    fw.dma("sp", self.Adram[:, :, ti, :].rearrange("c p f -> p c f"),
                           A[:].rearrange("p (c a) b -> p c (a b)", a=4), reads=[A], writes=[self.Adram.sub(ti)])
                fw.barrier()
            if "xn2b_d" in self.dbg:
                d_ = self.scratch("xn2b_d", [1024, NT], BF16)
                fw.dma("sp", kview(d_[:, :]), xn2b[:], reads=[xn2b], writes=[d_])
            if self.stopped(l, "peer1"):
                return
            with ExitStack() as ph0:
              oacc = fw.sb("oacc", [128, 8, NT], F32, ph0)
              with ExitStack() as ph:
                UT = [fw.sb(f"UT{i}", [128, 8, 512], BF16, ph) for i in range(2)]
                Vc = [fw.sb(f"Vc{i}", [128, 4, 1024], BF16, ph) for i in range(2)]
                Ac = [fw.sb(f"Ac{i}", [128, 18, 512], BF16, ph) for i in range(2)]
                Hg = [fw.sb(f"Hg{i}", [128, 512], BF16, ph) for i in range(2)]
                Gg = [fw.sb(f"Gg{i}", [128, 512], BF16, ph) for i in range(2)]
                GT = [fw.sb(f"GT{i}", [128, 4, 512], BF16, ph) for i in range(2)]
                nst = (ntile + 3) // 4

                def load_ec(ec):
                    u, v, a_ = UT[ec % 2], Vc[ec % 2], Ac[ec % 2]
                    fw.dma("pool", u[:], kview(self.uT[l, :, ec * 512:(ec + 1) * 512]), reads=[self.uT], writes=[u])
                    fw.dma("pool", v[:], self.pv[l, ec * 512:(ec + 1) * 512, :].rearrange("(s p) d -> p s d", p=128),
                           reads=[self.pv], writes=[v])
                    fw.dma("sp", a_[:, 0:ntile, :], self.Adram[ec, :, 0:ntile, :], reads=[self.Adram], writes=[a_])

                def emit_h(ec, ti):
                    u = UT[ec % 2]
                    ps = PS[ti % 2]
                    for k in range(8):
                        fw.op("pe", MM(ps[:, :], xn2b[:, k, ti * 128:(ti + 1) * 128], u[:, k, :], k == 0, k == 7), [xn2b, u], [ps])

                def emit_post(ec, ti):
                    a_ = Ac[ec % 2]
                    st, tl = ti // 4, ti % 4
                    gt = GT[st % 2]
                    ps = PS[ti % 2]
                    hg, gg = Hg[ti % 2], Gg[ti % 2]
                    fw.op("act", ACT(hg[:], ps[:, :], AF.Gelu_apprx_tanh), [ps], [hg])
                    fw.op("dve", TT(gg[:], hg[:], a_[:, ti, :], ALU.mult), [hg, a_], [gg])
                    pst = PS[2 + ti % 2]
                    pstb = pst[:].bitcast(BF16)
                    for s_ in range(4):
                        fw.op("pe", TR(pstb[:, s_ * 128:(s_ + 1) * 128], gg[:, s_ * 128:(s_ + 1) * 128], self.identb[:, :]),
                              [gg, self.identb], [pst])
                    fw.op("act", ACT(gt[:, :, tl * 128:(tl + 1) * 128], pstb[:, 0:512].rearrange("p (s n) -> p s n", s=4), AF.Copy),
                          [pst], [gt])

                def emit_out(ec, st):
                    v = Vc[ec % 2]
                    gt = GT[st % 2]
                    nn = (min(st * 4 + 4, ntile) - st * 4) * 128
                    n0 = st * 512
                    for dc in range(8):
                        ps = PS[4 + dc % 4]
                        for s_ in range(4):
                            fw.op("pe", MM(ps[:, :nn], v[:, s_, dc * 128:(dc + 1) * 128], gt[:, s_, :nn], s_ == 0, s_ == 3), [v, gt], [ps])
                        if ec == 0:
                            fw.op("dve", CP(oacc[:, dc, n0:n0 + nn], ps[:, :nn]), [ps], [oacc.sub((dc, st))])
                        else:
                            fw.op("dve", TT(oacc[:, dc, n0:n0 + nn], oacc[:, dc, n0:n0 + nn], ps[:, :nn], ALU.add),
                                  [ps, oacc.sub((dc, st))], [oacc.sub((dc, st))])

                jobs = [(ec, ti) for ec in range(32) for ti in range(ntile)]
                load_ec(0)
                emit_h(*jobs[0])
                for idx, (ec, ti) in enumerate(jobs):
                    if ti == 0 and ec + 1 < 32:
                        load_ec(ec + 1)
                    if idx + 1 < len(jobs):
                        emit_h(*jobs[idx + 1])
                    emit_post(ec, ti)
                    if ti % 4 == 3 or ti == ntile - 1:
                        emit_out(ec, ti // 4)
                fw.barrier()
              with ExitStack() as ph:
                xb = [fw.sb(f"pxo{i}", [128, 8, 512], F32, ph) for i in range(2)]
                for i, (n0, nb) in enumerate(BLK):
                    if n0 >= ntok:
                        continue
                    j = 0 if n0 < TL else 1
                    x = xb[i % 2]
                    fw.dma("sp", x[:, :, :nb], kview(self.xres[:, n0:n0 + nb]), reads=[self.xres], writes=[x])
                    for fc in range(8):
                        fw.op("dve", STT(x[:, fc, :nb], oacc[:, fc, n0:n0 + nb], self.modT[:, 40 + fc, j:j + 1], x[:, fc, :nb],
                                         ALU.mult, ALU.add), [oacc, self.modT, x], [x])
                    dst = self.outT if last else self.xres
                    fw.dma("sp", kview(dst[:, n0:n0 + nb]), x[:, :, :nb], reads=[x], writes=[dst.sub(("o", n0))])
                fw.barrier()


_CONST = {}


def _consts():
    if _CONST:
        return _CONST
    c = {}
    c["c_ident"] = np.eye(128, dtype=np.float32)
    rm = np.zeros((64, 64), np.float32)
    for d in range(64):
        if d % 32 < 16:
            rm[d + 16, d] = -1.0
            rm[d, d + 16] = 1.0
    c["c_rm"] = rm
    s = np.arange(64)[:, None]
    t = np.arange(64)[None, :]
    c["c_mask"] = np.stack([(s <= t), (s >= t)], axis=1).astype(np.float32)
    sel = np.zeros((4, 4, 64), np.float32)
    for h in range(4):
        sel[h, h, :] = 1.0
    c["c_sel"] = sel
    rst = np.ones((4, NT), np.float32)
    rst[:, ::64] = 0.0
    c["c_reset"] = rst
    tt = np.arange(TL)
    row = (tt // 64).astype(np.float32)
    col = (tt % 64).astype(np.float32)
    inv = (np.float32(10000.0) ** (-np.arange(16, dtype=np.float32) / np.float32(16))).astype(np.float32)
    cosT = np.zeros((64, TL), np.float32)
    sinT = np.zeros((64, TL), np.float32)
    for d in range(64):
        ax = d // 32
        f = d % 16
        ang = (row if ax == 0 else col) * inv[f]
        cosT[d] = np.cos(ang.astype(np.float32))
        sinT[d] = np.sin(ang.astype(np.float32))
    c["c_cos"] = cosT
    c["c_sin"] = sinT
    ic = np.zeros((128, 2, LP), np.float32)
    for g, w in enumerate((2, 4, 8, 16)):
        for (T_, off) in ((TL, PL0), (TC, PC0)):
            t_ = np.arange(T_)
            lo = np.clip(t_ - w // 2, 0, T_ - 1)
            hi = np.clip(t_ + (w - w // 2 - 1), 0, T_ - 1)
            cnt = (hi - lo + 1).astype(np.float32)
            ic[(g % 2) * 64:(g % 2) * 64 + 64, g // 2, off:off + T_] = (np.float32(1.0) / cnt)[None, :]
    c["c_ic"] = ic
    _CONST.update(c)
    return _CONST


def _prep_shared(inp):
    f = lambda a: np.ascontiguousarray(np.asarray(a, dtype=np.float32))
    sh = {}
    sh["w_ada"] = f(inp["w_ada"])
    sh["bT"] = f(inp["b_ada"].reshape(2, 48, 128).transpose(0, 2, 1))
    sh["n1g"] = f(inp["norm1_g"].reshape(2, 8, 128).transpose(0, 2, 1))
    sh["n2g"] = f(inp["norm2_g"].reshape(2, 8, 128).transpose(0, 2, 1))
    sh["w_in"] = f(inp["w_in"])
    sh["w_out"] = f(inp["w_out"])
    sh["nag"] = f(np.stack([inp["na_q_g"], inp["na_k_g"]], axis=-1))
    rpb = np.asarray(inp["na_rpb"], np.float32)
    w = np.arange(64)[None, :]
    x = np.arange(64)[:, None]
    c0 = np.clip(w - 8, 0, 48)
    inwin = (x >= c0) & (x < c0 + 16)
    bcol = np.clip(x - w + 15, 0, 30)
    tab = rpb[:, :, :, bcol]
    tab = np.where(inwin[None, None, None], tab, np.float32(-30000.0)).astype(np.float32)
    bt = np.stack([np.concatenate([tab[:, :, s], tab[:, :, s + 1]], axis=2) for s in range(14)], axis=2)
    sh["rpbT"] = f(bt)
    pw = np.asarray(inp["pool_w"], np.float32)
    bd = np.zeros((2, 2, 128, 128), np.float32)
    for g in range(4):
        bd[:, g // 2, (g % 2) * 64:(g % 2) * 64 + 64, (g % 2) * 64:(g % 2) * 64 + 64] = pw[:, g]
    sh["pool_wbd"] = bd
    sh["pool_sc"] = f(inp["pool_scale"].reshape(2, 2, 128).transpose(0, 2, 1))
    cv = np.asarray(inp["ml_conv"], np.float32)
    sh["convw"] = f(cv.reshape(2, 5, 8, 64).transpose(0, 3, 2, 1))
    sh["ml_gb"] = f(inp["ml_gate_b"].reshape(2, 4, 4).transpose(0, 2, 1))
    sh["ml_ng"] = f(inp["ml_norm_g"].reshape(2, 4, 64).transpose(0, 2, 1))
    sh["wq"] = f(inp["peer_wq"])
    pk = np.asarray(inp["peer_keys"], np.float32)
    sh["keysT"] = f(pk.reshape(2, 16, 128, 128).transpose(0, 1, 3, 2))
    sh["uT"] = f(np.asarray(inp["peer_u"], np.float32).transpose(0, 2, 1))
    sh["pv"] = f(inp["peer_v"])
    sh.update(_consts())
    return sh


def _prep_core(inp, b):
    d = {}
    xx = np.concatenate([np.asarray(inp["x"][b], np.float32), np.asarray(inp["ctx"][b], np.float32)], axis=0)
    d["xT"] = np.ascontiguousarray(xx.T)
    cc = np.stack([np.asarray(inp["c"][b], np.float32), np.asarray(inp["c_ctx"], np.float32)], axis=-1)
    d["cT"] = np.ascontiguousarray(cc.reshape(8, 128, 2).transpose(1, 0, 2))
    return d


def kernel(**inputs):
    nc = K().build()
    sh = _prep_shared(inputs)
    in_maps = []
    for b in range(8):
        m = dict(sh)
        m.update(_prep_core(inputs, b))
        in_maps.append(m)
    res = run_bass_kernel_spmd(nc, in_maps, core_ids=list(range(8)))
    out = np.stack([np.ascontiguousarray(np.asarray(r["outT"], np.float32).T) for r in res.results], axis=0)
    return out.astype(np.float32)
```
